# Optimizing a Trainium2 kernel written in Bass

```python
import jax, jax.numpy as jnp
from jax import lax
import numpy as np


D_MODEL = 1024
BATCH = 32
SEQ = 2048
DEPTH = 2
DEC_BATCH = 16
DEC_SEQ = 4096
PAST_LEN = 128

A_HEADS = 4
A_HEAD_DIM = 128
A_WIDTH = A_HEADS * A_HEAD_DIM
A_CHUNK = 64
B_GROUPS = ((128, 1), (512, 4), (2048, 16))
B_HEADS_PER_GROUP = 4
B_HEADS = B_HEADS_PER_GROUP * len(B_GROUPS)
B_HEAD_DIM = 64
B_WIDTH = B_HEADS * B_HEAD_DIM
B_QBLOCK = 64
C_WIDTH = D_MODEL
CONV_WIDTH = 31
D_FF = 4 * D_MODEL
REL_BUCKETS = 32
REL_MAX_DIST = 1024
AB_IN = 5 * A_WIDTH + 3 * B_WIDTH
AB_OUT = A_WIDTH + B_WIDTH
N_AB = (DEPTH + 1) // 2
N_C = DEPTH // 2
ALPHA = (2 * DEPTH) ** 0.25
BETA = (8 * DEPTH) ** -0.25
LN_EPS = 1e-5
RMS_EPS = 1e-6
NEG_INF = -1e30

kernel_name = 'hybrid_hgrn2_dilated_conformer_encoder'


def layer_norm(x, g, b):
    xf = x.astype(jnp.float32)
    mu = jnp.mean(xf, -1, keepdims=True)
    var = jnp.mean(jnp.square(xf - mu), -1, keepdims=True)
    y = (xf - mu) * lax.rsqrt(var + LN_EPS) * g.astype(jnp.float32) + b.astype(jnp.float32)
    return y.astype(x.dtype)


def t5_buckets(rel):
    half = REL_BUCKETS // 2
    max_exact = half // 2
    n = np.abs(rel)
    large = max_exact + (np.log(np.maximum(n, 1) / max_exact) / np.log(REL_MAX_DIST / max_exact)
                         * (half - max_exact)).astype(np.int32)
    large = np.minimum(large, half - 1)
    return (np.where(rel > 0, half, 0) + np.where(n < max_exact, n, large)).astype(np.int32)


def hgrn2_scan(q, k, v, logf):
    Bsz, S, H, dk = q.shape
    dv = v.shape[-1]
    C = A_CHUNK
    n = S // C

    def chunks(t):
        return t.reshape(Bsz, n, C, H, t.shape[-1]).transpose(1, 0, 3, 2, 4)

    lower = jnp.tril(jnp.ones((C, C), dtype=bool))[:, :, None]

    def step(state, inp):
        qc, kc, vc, lc = inp
        b = jnp.cumsum(lc, axis=2)
        o_inter = jnp.einsum('bhtd,bhdv->bhtv', qc * jnp.exp(b), state)
        diff = jnp.where(lower, b[:, :, :, None, :] - b[:, :, None, :, :], -jnp.inf)
        scores = jnp.einsum('bhtd,bhsd,bhtsd->bhts', qc, kc, jnp.exp(diff))
        o_intra = jnp.einsum('bhts,bhsv->bhtv', scores, vc)
        b_last = b[:, :, -1:, :]
        new_state = (jnp.exp(b_last[:, :, 0, :, None]) * state
                     + jnp.einsum('bhsd,bhsv->bhdv', kc * jnp.exp(b_last - b), vc))
        return new_state, o_inter + o_intra

    state0 = jnp.zeros((Bsz, H, dk, dv), jnp.float32)
    _, o = lax.scan(step, state0, (chunks(q), chunks(k), chunks(v), chunks(logf)))
    return o.transpose(1, 0, 3, 2, 4).reshape(Bsz, S, H, dv)


def hgrn2_mixer(xa, g_norm, lb):
    q, i, zf, zb, g = jnp.split(xa, 5, axis=-1)
    Bsz, S, _ = q.shape

    def heads(t):
        return t.reshape(Bsz, S, A_HEADS, A_HEAD_DIM)

    qh = heads(jax.nn.silu(q.astype(jnp.float32)))
    vh = heads(i.astype(jnp.float32))

    def gates(z, lb_d):
        z = z.astype(jnp.float32)
        logf = jnp.log(lb_d + (1.0 - lb_d) * jax.nn.sigmoid(z))
        k = (1.0 - lb_d) * jax.nn.sigmoid(-z)
        return heads(k), heads(logf)

    kf, lf = gates(zf, lb[0])
    kb, lbw = gates(zb, lb[1])
    o_f = hgrn2_scan(qh, kf, vh, lf)

    def flip(t):
        return jnp.flip(t, axis=1)

    o_b = flip(hgrn2_scan(flip(qh), flip(kb), flip(vh), flip(lbw)))
    o = o_f + o_b
    o = o * lax.rsqrt(jnp.mean(o * o, -1, keepdims=True) + RMS_EPS)
    o = o.reshape(Bsz, S, A_WIDTH) * g_norm.astype(jnp.float32) * jax.nn.silu(g.astype(jnp.float32))
    return o.astype(xa.dtype)


def dilated_group(q, k, v, bias_table, window, r):
    Bsz, S, H, dh = q.shape
    QB = B_QBLOCK
    half = window // (2 * r)
    L = S // r
    nb = -(-L // QB)
    Lp = nb * QB

    def sub(t):
        return t.reshape(Bsz, L, r, H, dh).transpose(0, 2, 3, 1, 4)

    qb = jnp.pad(sub(q), ((0, 0), (0, 0), (0, 0), (0, Lp - L), (0, 0))).reshape(Bsz, r, H, nb, QB, dh)

    def band(t):
        tp = jnp.pad(sub(t), ((0, 0), (0, 0), (0, 0), (QB, Lp - L + QB), (0, 0)))
        tp = tp.reshape(Bsz, r, H, nb + 2, QB, dh)
        return jnp.concatenate([tp[:, :, :, :-2], tp[:, :, :, 1:-1], tp[:, :, :, 2:]], axis=4)

    kb, vb = band(k), band(v)
    a = np.arange(QB)[:, None]
    c = np.arange(3 * QB)[None, :]
    rel = c - QB - a
    key_idx = np.arange(nb)[:, None, None] * QB - QB + c[None]
    valid = (np.abs(rel) <= half)[None] & (key_idx >= 0) & (key_idx < L)
    bias = jnp.transpose(bias_table[t5_buckets(rel * r)], (2, 0, 1)).astype(jnp.float32)
    s = jnp.einsum('brhnqd,brhnkd->brhnqk', qb, kb) * (dh ** -0.5) + bias[:, None]
    s = jnp.where(valid, s, NEG_INF)
    m = jnp.max(s, -1, keepdims=True)
    p = jnp.exp(s - m)
    den = jnp.sum(p, -1, keepdims=True)
    o = jnp.einsum('brhnqk,brhnkd->brhnqd', p, vb) / den
    lse = (m + jnp.log(den))[..., 0]
    o = o.reshape(Bsz, r, H, Lp, dh)[:, :, :, :L].transpose(0, 3, 1, 2, 4).reshape(Bsz, S, H, dh)
    lse = lse.reshape(Bsz, r, H, Lp)[:, :, :, :L].transpose(0, 3, 1, 2).reshape(Bsz, S, H)
    return o, lse


def dilated_mixer(xb, rel_bias):
    Bsz, S, _ = xb.shape
    q, k, v = [t.astype(jnp.float32).reshape(Bsz, S, B_HEADS, B_HEAD_DIM) for t in jnp.split(xb, 3, axis=-1)]
    outs, lses = [], []
    for gi, (window, r) in enumerate(B_GROUPS):
        hs = slice(gi * B_HEADS_PER_GROUP, (gi + 1) * B_HEADS_PER_GROUP)
        o, lse = dilated_group(q[:, :, hs], k[:, :, hs], v[:, :, hs], rel_bias[:, hs], window, r)
        outs.append(o)
        lses.append(lse)
    o = jnp.stack(outs, axis=2)
    weight = jax.nn.softmax(jnp.stack(lses, axis=2), axis=2)
    return (o * weight[..., None]).reshape(Bsz, S, B_WIDTH).astype(xb.dtype)


def ab_mixer(x, w_in, g_norm, w_out, rel_bias, lb):
    h = x @ w_in
    o_a = hgrn2_mixer(h[..., :5 * A_WIDTH], g_norm, lb)
    o_b = dilated_mixer(h[..., 5 * A_WIDTH:], rel_bias)
    return jnp.concatenate([o_a, o_b], axis=-1) @ w_out


def conv_mixer(x, w_in, b_in, dw_w, dw_b, n_g, n_b, w_out, b_out):
    h = x @ w_in + b_in
    a, gate = jnp.split(h, 2, axis=-1)
    u = a * jax.nn.sigmoid(gate)
    u = lax.conv_general_dilated(u, dw_w[:, None, :].astype(u.dtype), window_strides=(1,), padding='SAME',
                                 dimension_numbers=('NWC', 'WIO', 'NWC'),
                                 feature_group_count=C_WIDTH) + dw_b
    u = jax.nn.silu(layer_norm(u, n_g, n_b))
    return u @ w_out + b_out


def sq_relu_mlp(x, w1, w2):
    return jnp.square(jax.nn.relu(x @ w1)) @ w2


def trunk(x, rel_bias, hgrn_lb, w_in_ab, hgrn_norm, w_out_ab, w_in_c, b_in_c, dw_c, dw_b_c, cnorm_g,
          cnorm_b, w_out_c, b_out_c, ln_mix_g, ln_mix_b, mlp_w1, mlp_w2, ln_ffn_g, ln_ffn_b):
    lb_all = jnp.cumsum(jax.nn.softmax(hgrn_lb.astype(jnp.float32), axis=1), axis=1)
    for l in range(DEPTH):
        i = l // 2
        if l % 2 == 0:
            mix = ab_mixer(x, w_in_ab[i], hgrn_norm[i], w_out_ab[i], rel_bias, lb_all[:, l])
        else:
            mix = conv_mixer(x, w_in_c[i], b_in_c[i], dw_c[i], dw_b_c[i], cnorm_g[i], cnorm_b[i],
                             w_out_c[i], b_out_c[i])
        x = layer_norm(ALPHA * x + mix, ln_mix_g[l], ln_mix_b[l])
        x = layer_norm(ALPHA * x + sq_relu_mlp(x, mlp_w1[l], mlp_w2[l]), ln_ffn_g[l], ln_ffn_b[l])
    return x


def setup_inputs(seed: int = 0) -> dict:
    key = jax.random.key(seed)
    ks = jax.random.split(key, 21)

    def nrm(k, shape, scale):
        return jax.random.normal(k, shape, jnp.float32) * scale

    return {
        'x_prompt': nrm(ks[0], (BATCH, SEQ, D_MODEL), 1.0),
        'x_sample': nrm(ks[1], (DEC_BATCH, DEC_SEQ, D_MODEL), 1.0),
        'rel_bias': nrm(ks[2], (REL_BUCKETS, B_HEADS), 0.5),
        'hgrn_lb': nrm(ks[3], (2, DEPTH + 1, A_WIDTH), 0.5),
        'w_in_ab': nrm(ks[4], (N_AB, D_MODEL, AB_IN), D_MODEL ** -0.5),
        'hgrn_norm': 1.0 + nrm(ks[5], (N_AB, A_WIDTH), 0.05),
        'w_out_ab': nrm(ks[6], (N_AB, AB_OUT, D_MODEL), BETA * AB_OUT ** -0.5),
        'w_in_c': nrm(ks[7], (N_C, D_MODEL, 2 * C_WIDTH), D_MODEL ** -0.5),
        'b_in_c': nrm(ks[8], (N_C, 2 * C_WIDTH), 0.02),
        'dw_c': nrm(ks[9], (N_C, CONV_WIDTH, C_WIDTH), CONV_WIDTH ** -0.5),
        'dw_b_c': nrm(ks[10], (N_C, C_WIDTH), 0.02),
        'cnorm_g': 1.0 + nrm(ks[11], (N_C, C_WIDTH), 0.05),
        'cnorm_b': nrm(ks[12], (N_C, C_WIDTH), 0.02),
        'w_out_c': nrm(ks[13], (N_C, C_WIDTH, D_MODEL), BETA * C_WIDTH ** -0.5),
        'b_out_c': nrm(ks[14], (N_C, D_MODEL), 0.02),
        'ln_mix_g': 1.0 + nrm(ks[15], (DEPTH, D_MODEL), 0.05),
        'ln_mix_b': nrm(ks[16], (DEPTH, D_MODEL), 0.02),
        'mlp_w1': nrm(ks[17], (DEPTH, D_MODEL, D_FF), D_MODEL ** -0.5),
        'mlp_w2': nrm(ks[18], (DEPTH, D_FF, D_MODEL), BETA * D_FF ** -0.5),
        'ln_ffn_g': 1.0 + nrm(ks[19], (DEPTH, D_MODEL), 0.05),
        'ln_ffn_b': nrm(ks[20], (DEPTH, D_MODEL), 0.02),
    }


def reference(x_prompt, x_sample, rel_bias, hgrn_lb, w_in_ab, hgrn_norm, w_out_ab, w_in_c, b_in_c, dw_c,
              dw_b_c, cnorm_g, cnorm_b, w_out_c, b_out_c, ln_mix_g, ln_mix_b, mlp_w1, mlp_w2, ln_ffn_g,
              ln_ffn_b):
    y_prompt = trunk(x_prompt, rel_bias, hgrn_lb, w_in_ab, hgrn_norm, w_out_ab, w_in_c, b_in_c, dw_c, dw_b_c,
                     cnorm_g, cnorm_b, w_out_c, b_out_c, ln_mix_g, ln_mix_b, mlp_w1, mlp_w2, ln_ffn_g, ln_ffn_b)
    y_sample = trunk(x_sample, rel_bias, hgrn_lb, w_in_ab, hgrn_norm, w_out_ab, w_in_c, b_in_c, dw_c, dw_b_c,
                     cnorm_g, cnorm_b, w_out_c, b_out_c, ln_mix_g, ln_mix_b, mlp_w1, mlp_w2, ln_ffn_g, ln_ffn_b)
    return (y_prompt, y_sample)
```

```python
import numpy as np
import concourse.bass as bass
import concourse.mybir as mybir
from concourse.bass_utils import run_bass_kernel_spmd

F32 = mybir.dt.float32
BF16 = mybir.dt.bfloat16
AF = mybir.ActivationFunctionType
ALU = mybir.AluOpType
AX = mybir.AxisListType

D = 1024
DFF = 4096
ALPHA = 4.0 ** 0.25
LN_EPS = 1e-5
RMS_EPS = 1e-6
TT = 256
NEG = -30000.0


class Eng:
    def __init__(self, fw, name, eng, own_wait):
        self.name = name
        self.eng = eng
        self.count = 0
        self.waited = {}
        self.sem = fw.nc.alloc_semaphore(name="sem_" + name)
        self.own_wait = own_wait


class Buf:
    __slots__ = ("name", "w", "r", "sem", "cnt", "multi")

    def __init__(self, name="", sem=None, multi=False):
        self.name = name
        self.multi = multi
        self.w = {}
        self.r = {}
        self.sem = sem
        self.cnt = 0


class FW:
    def __init__(self, nc):
        self.nc = nc
        self.PE = Eng(self, "pe", nc.tensor, False)
        self.ACT = Eng(self, "act", nc.scalar, True)
        self.DVE = Eng(self, "dve", nc.vector, True)
        self.POOL = Eng(self, "pool", nc.gpsimd, True)
        self.SP = Eng(self, "sp", nc.sync, True)
        self.engs = [self.PE, self.ACT, self.DVE, self.POOL, self.SP]
        self.dbufs = []
        self.nsem = 0
        self.pool = []
        self.phase_bufs = []
        self.quiet = {}
        self.rec = None

    def buf(self, name="", dma=False, multi=False):
        if not dma:
            return Buf(name, None, multi)
        if self.pool:
            sem, cnt = self.pool.pop()
        else:
            sem = self.nc.alloc_semaphore(name="dsem_%d" % self.nsem)
            self.nsem += 1
            cnt = 0
        b = Buf(name, sem)
        b.cnt = cnt
        self.dbufs.append(b)
        self.phase_bufs.append(b)
        return b

    def end_phase(self):
        self.barrier()
        for b in self.phase_bufs:
            self.pool.append((b.sem, b.cnt))
            self.dbufs.remove(b)
        self.phase_bufs = []

    def _wait(self, E, tok):
        sem, val = tok
        if sem is E.sem and not E.own_wait:
            return
        key = id(sem)
        if E.waited.get(key, 0) >= val:
            return
        E.waited[key] = val
        E.eng.wait_ge(sem, val)

    def _deps(self, E, reads, writes):
        for b in reads:
            for t in b.w.values():
                self._wait(E, t)
        for b in writes:
            for t in b.w.values():
                self._wait(E, t)
            for t in b.r.values():
                self._wait(E, t)

    def _mark(self, tok, reads, writes):
        s, v = tok
        for b in reads:
            b.r[id(s)] = (s, v)
        for b in writes:
            if b.multi:
                b.w[id(s)] = tok
            else:
                b.w = {id(s): tok}
            b.r = {}

    def record(self, fn):
        self.rec = []
        fn()
        r, self.rec = self.rec, None
        return r

    def emit_interleaved(self, lists):
        lists = [list(l) for l in lists if l]
        pos = [0] * len(lists)
        live = True
        while live:
            live = False
            for i, l in enumerate(lists):
                if pos[i] < len(l):
                    l[pos[i]]()
                    pos[i] += 1
                    live = True

    def op(self, E, fn, reads=(), writes=()):
        if self.rec is not None:
            self.rec.append(lambda: self.op(E, fn, reads, writes))
            return
        self._deps(E, reads, writes)
        ins = fn(E.eng)
        E.count += 1
        ins.then_inc(E.sem, 1)
        self._mark((E.sem, E.count), reads, writes)

    def mm(self, fns, reads=(), writes=()):
        if self.rec is not None:
            fns = list(fns)
            self.rec.append(lambda: self.mm(fns, reads, writes))
            return
        E = self.PE
        self._deps(E, reads, writes)
        ins = None
        for fn in fns:
            ins = fn(E.eng)
        E.count += 1
        ins.then_inc(E.sem, 1)
        self._mark((E.sem, E.count), reads, writes)

    def dma(self, Q, pairs, reads=(), writes=(), semb=None, **kw):
        if self.rec is not None:
            pairs = list(pairs)
            self.rec.append(lambda: self.dma(Q, pairs, reads, writes, semb, **kw))
            return
        self._deps(Q, reads, writes)
        if semb.cnt > 0:
            self._wait(Q, (semb.sem, semb.cnt))
        for (o, i) in pairs:
            ins = Q.eng.dma_start(out=o, in_=i, **kw)
            semb.cnt += 16
            ins.then_inc(semb.sem, 16)
        self._mark((semb.sem, semb.cnt), reads, writes)

    def barrier(self):
        toks = [(E.sem, E.count) for E in self.engs if E.count > 0]
        toks += [(b.sem, b.cnt) for b in self.dbufs if b.cnt > 0]
        for E in self.engs:
            for t in toks:
                self._wait(E, t)


class Ctx:
    pass


def sb(c, name, shape, dt):
    rp = getattr(c, "replay", None)
    if rp is not None:
        i = rp["i_sb"]
        rp["i_sb"] += 1
        if i < len(rp["sb"]):
            ap, shp, d0 = rp["sb"][i]
            assert shp == list(shape) and d0 == dt, (name, shp, shape)
            return ap
        c.replay = None
        ap = sb(c, name, shape, dt)
        c.replay = rp
        rp["sb"].append((ap, list(shape), dt))
        return ap
    c.uid += 1
    h = c.es.enter_context(c.nc.sbuf_tensor("%s_%d" % (name, c.uid), list(shape), dt))
    return h.ap() if hasattr(h, "ap") else h[:]


def load_w_bf16(c, dst, src2d, semb, col_block=2048):
    fw = c.fw
    K, N = src2d.shape
    v = src2d.rearrange("(k p) n -> p k n", p=128)
    cb = min(N, 1024)
    i = 0
    for k in range(K // 128):
        for n0 in range(0, N, cb):
            st, b_st = c.wstage[i % 2], c.b_wstage[i % 2]
            i += 1
            fw.dma(fw.SP, [(st[:, 0:cb], v[:, k, n0:n0 + cb])], writes=[b_st], semb=b_st)
            fw.op(fw.POOL, lambda e, st=st, k=k, n0=n0: e.tensor_copy(out=dst[:, k, n0:n0 + cb], in_=st[:, 0:cb]),
                  reads=[b_st], writes=[semb])


def w_chunks(c, dst, src2d, b_dst):
    fw = c.fw
    K, N = src2d.shape
    v = src2d.rearrange("(k p) n -> p k n", p=128)
    out = []
    cnt = [0]
    for k in range(K // 128):
        for n0 in range(0, N, 1024):
            def thunk(k=k, n0=n0):
                st, b_st = c.wstage[cnt[0] % 2], c.b_wstage[cnt[0] % 2]
                cnt[0] += 1
                fw.dma(fw.SP, [(st[:, 0:1024], v[:, k, n0:n0 + 1024])], writes=[b_st], semb=b_st)
                fw.op(fw.POOL, lambda e: e.tensor_copy(out=dst[:, k, n0:n0 + 1024], in_=st[:, 0:1024]), reads=[b_st], writes=[b_dst])
            out.append(thunk)
    return out


def load_bc(c, dst, src1d, semb):
    fw = c.fw
    n = src1d.shape[0]
    src = bass.AP(src1d.tensor, src1d.offset, [[0, 128], [1, n]])
    fw.dma(fw.SP, [(dst, src)], writes=[semb], semb=semb)


def load_pp(c, dst, src1d, semb):
    fw = c.fw
    v = src1d.rearrange("(c p) -> p c", p=128)
    fw.dma(fw.SP, [(dst, v)], writes=[semb], semb=semb, allow_slow_non_contiguous=True)


def interleave(gens):
    gens = list(gens)
    while gens:
        for g_ in list(gens):
            try:
                next(g_)
            except StopIteration:
                gens.remove(g_)


def ln_chain(c, slot, ypre, b_ypre, gbc, bbc, b_gb, xdst_rows, xT_stage, b_xT, sub, b_dram_x, tail=True):
    fw = c.fw
    st, mv, xb, pt = c.ln_stats[slot], c.ln_mv[slot], c.ln_xb[slot], c.ps_tr2[slot]
    b_st, b_mv, b_xb, b_pt = c.b_lnst[slot], c.b_lnmv[slot], c.b_lnxb[slot], c.b_ps_tr2[slot]
    fw.op(fw.DVE, lambda e: e.bn_stats(out=st[:, 0:6], in_=ypre[:, 0:512]), reads=[b_ypre], writes=[b_st])
    yield
    fw.op(fw.DVE, lambda e: e.bn_stats(out=st[:, 6:12], in_=ypre[:, 512:1024]), reads=[b_ypre], writes=[b_st])
    yield
    fw.op(fw.DVE, lambda e: e.bn_aggr(out=mv[:, 0:2], in_=st[:, 0:12]), reads=[b_st], writes=[b_mv])
    yield
    fw.op(fw.ACT, lambda e: e.activation(out=mv[:, 3:4], in_=mv[:, 1:2], func=AF.Ln, bias=c.lneps_col[:, 0:1]),
          reads=[b_mv, c.b_const], writes=[b_mv])
    yield
    fw.op(fw.ACT, lambda e: e.activation(out=mv[:, 4:5], in_=mv[:, 3:4], func=AF.Exp, scale=-0.5), reads=[b_mv], writes=[b_mv])
    yield
    fw.op(fw.DVE, lambda e: e.tensor_scalar(out=ypre, in0=ypre, scalar1=mv[:, 0:1], scalar2=mv[:, 4:5],
                                            op0=ALU.subtract, op1=ALU.mult), reads=[b_ypre, b_mv], writes=[b_ypre])
    yield
    fw.op(fw.POOL, lambda e: e.tensor_tensor(out=ypre, in0=ypre, in1=gbc, op=ALU.mult), reads=[b_ypre, b_gb], writes=[b_ypre])
    yield
    fw.op(fw.POOL, lambda e: e.tensor_tensor(out=ypre, in0=ypre, in1=bbc, op=ALU.add), reads=[b_ypre, b_gb], writes=[b_ypre])
    yield
    fw.dma(fw.SP, [(xdst_rows, ypre)], reads=[b_ypre], writes=[b_dram_x], semb=b_ypre)
    yield
    if xT_stage is None:
        return
    fw.op(fw.ACT, lambda e: e.activation(out=xb, in_=ypre, func=AF.Copy), reads=[b_ypre], writes=[b_xb])
    yield
    if tail:
        yield from ln_tail(c, slot, xT_stage, b_xT, sub)


def ln_tail(c, slot, xT_stage, b_xT, sub):
    fw = c.fw
    xb, pt = c.ln_xb[slot], c.ps_tr2[slot]
    b_xb, b_pt = c.b_lnxb[slot], c.b_ps_tr2[slot]
    fw.mm([(lambda e, k=k: e.transpose(out=pt[:, k * 128:(k + 1) * 128], in_=xb[:, k * 128:(k + 1) * 128], identity=c.ident))
           for k in range(8)], reads=[b_xb, c.b_const], writes=[b_pt])
    yield
    fw.op(fw.DVE, lambda e: e.tensor_copy(out=xT_stage[:, :, sub * 128:(sub + 1) * 128],
                                          in_=pt.rearrange("p (k t) -> p k t", k=8)),
          reads=[b_pt], writes=[b_xT])
    yield


def setup_consts(c):
    fw, nc = c.fw, c.nc
    c.ident = sb(c, "ident", [128, 128], BF16)
    c.b_const = fw.buf("const")
    fw.op(fw.POOL, lambda e: e.memset(c.ident, 0.0), writes=[c.b_const])
    fw.op(fw.POOL, lambda e: e.affine_select(out=c.ident, in_=c.ident, pattern=[[-1, 128]], compare_op=ALU.not_equal,
                                             fill=1.0, base=0, channel_multiplier=1), reads=[c.b_const], writes=[c.b_const])
    c.ones16 = sb(c, "ones16", [128, 128], BF16)
    fw.op(fw.POOL, lambda e: e.memset(c.ones16, 1.0), writes=[c.b_const])
    c.lneps_col = sb(c, "lneps_col", [128, 1], F32)
    fw.op(fw.POOL, lambda e: e.memset(c.lneps_col, LN_EPS), writes=[c.b_const])
    c.wstage = [sb(c, "wstage%d" % i, [128, 1024], F32) for i in range(2)]
    c.b_wstage = [fw.buf("wst", dma=True) for i in range(2)]
    fw.phase_bufs = []
    c.ln_stats = [sb(c, "ln_stats%d" % i, [128, 12], F32) for i in range(2)]
    c.ln_mv = [sb(c, "ln_mv%d" % i, [128, 8], F32) for i in range(2)]
    c.ln_xb = [sb(c, "ln_xb%d" % i, [128, 1024], BF16) for i in range(2)]
    c.b_lnst = [fw.buf() for i in range(2)]
    c.b_lnmv = [fw.buf() for i in range(2)]
    c.b_lnxb = [fw.buf() for i in range(2)]
    c.ps = [nc.alloc_psum_tensor("ps%d" % i, [128, 512], F32).ap() for i in range(8)]
    c.b_ps = [fw.buf("ps%d" % i) for i in range(8)]
    c.ps_tr = c.ps[7].bitcast(BF16)
    c.b_ps_tr = c.b_ps[7]
    c.ps_tr2 = [c.ps[6].bitcast(BF16), c.ps[7].bitcast(BF16)]
    c.b_ps_tr2 = [c.b_ps[6], c.b_ps[7]]


def phase_B2(c, layer, XT_in, dr_XT_in, X_in, dr_X_in, X_out_rows, dr_X_out, XT_out, dr_XT_out, w1, b_w1, w1_todo):
    fw, nc = c.fw, c.nc
    nt = c.T // TT
    w2 = sb(c, "w2_%d" % layer, [128, 32, D], BF16)
    b_w2 = fw.buf("w2", dma=True)
    while w1_todo:
        w1_todo.pop(0)()
    load_w_bf16(c, w2, c.din["mlp_w2"][layer], b_w2, col_block=1024)
    gbc = sb(c, "b2g_%d" % layer, [128, D], F32)
    bbc = sb(c, "b2b_%d" % layer, [128, D], F32)
    b_gb = fw.buf("gb", dma=True)
    load_bc(c, gbc, c.din["ln_ffn_g"][layer], b_gb)
    load_bc(c, bbc, c.din["ln_ffn_b"][layer], b_gb)
    NB = 2
    xT = [sb(c, "b2xT%d_%d" % (i, layer), [128, 8, TT], BF16) for i in range(NB)]
    xr = [sb(c, "b2xr%d_%d" % (i, layer), [128, 2, D], F32) for i in range(NB)]
    b_xT = [fw.buf("b2xT", dma=True) for i in range(NB)]
    b_xr = [fw.buf("b2xr", dma=True) for i in range(NB)]
    h1 = sb(c, "b2h1_%d" % layer, [128, 32, TT], BF16)
    b_h1 = [fw.buf() for f in range(32)]
    rl = [sb(c, "b2rl%d_%d" % (i, layer), [128, TT], F32) for i in range(2)]
    b_rl = [fw.buf() for i in range(2)]
    yp = [sb(c, "b2yp%d_%d" % (i, layer), [128, D], F32) for i in range(2)]
    b_yp = [fw.buf("b2yp", dma=True) for i in range(2)]
    xTo = [sb(c, "b2xTo%d_%d" % (i, layer), [128, 8, TT], BF16) for i in range(1)] * 2
    b_xTo = [fw.buf("b2xTo", dma=True) for i in range(1)] * 2
    Xrows = X_in.rearrange("(n s p) d -> n p s d", s=2, p=128)

    def load(i):
        s = i % NB
        fw.dma(fw.SP, [(xT[s], XT_in[i])], reads=[dr_XT_in[i]], writes=[b_xT[s]], semb=b_xT[s])
        fw.dma(fw.SP, [(xr[s], Xrows[i])], reads=[dr_X_in[i]], writes=[b_xr[s]], semb=b_xr[s])

    def tails(i):
        interleave([ln_tail(c, 0, xTo[0], b_xTo[0], 0), ln_tail(c, 1, xTo[0], b_xTo[0], 1)])
        fw.dma(fw.SP, [(XT_out[i], xTo[0])], reads=[b_xTo[0]], writes=[dr_XT_out[i]], semb=b_xTo[0])

    load(0)
    hcnt = 0
    ycnt = 0
    for i in range(nt):
        if i + 1 < nt:
            load(i + 1)
        s = i % NB
        for f in range(32):
            pi = hcnt % 2
            hcnt += 1
            ph = c.ps[pi][:, 0:TT]
            fw.mm([(lambda e, k=k, f=f, ph=ph: e.matmul(ph, lhsT=w1[:, k, f * 128:(f + 1) * 128], rhs=xT[s][:, k, :],
                                                        start=(k == 0), stop=(k == 7))) for k in range(8)],
                  reads=[b_w1, b_xT[s]], writes=[c.b_ps[pi]])
            fw.op(fw.ACT, lambda e, ph=ph, pi=pi: e.activation(out=rl[pi], in_=ph, func=AF.Relu),
                  reads=[c.b_ps[pi]], writes=[b_rl[pi]])
            fw.op(fw.DVE, lambda e, pi=pi, f=f: e.tensor_tensor(out=h1[:, f, :], in0=rl[pi], in1=rl[pi], op=ALU.mult),
                  reads=[b_rl[pi]], writes=[b_h1[f]])
        if XT_out is not None and i > 0:
            tails(i - 1)

        def subgen(sub, i=i, s=s):
            yi = sub
            for half in range(2):
                pj = 2 + sub * 2 + half
                py = c.ps[pj]
                fw.mm([(lambda e, f=f, py=py, half=half, sub=sub: e.matmul(
                    py, lhsT=h1[:, f, sub * 128:(sub + 1) * 128], rhs=w2[:, f, half * 512:(half + 1) * 512],
                    start=(f == 0), stop=(f == 31))) for f in range(32)],
                    reads=[b_w2] + b_h1, writes=[c.b_ps[pj]])
                yield
                fw.op(fw.DVE, lambda e, py=py, half=half, sub=sub, yi=yi: e.scalar_tensor_tensor(
                    out=yp[yi][:, half * 512:(half + 1) * 512], in0=xr[s][:, sub, half * 512:(half + 1) * 512],
                    scalar=ALPHA, in1=py, op0=ALU.mult, op1=ALU.add),
                    reads=[c.b_ps[pj], b_xr[s]], writes=[b_yp[yi]])
                yield
            t0 = i * TT + sub * 128
            yield from ln_chain(c, sub, yp[yi], b_yp[yi], gbc, bbc, b_gb, X_out_rows(t0),
                                None if XT_out is None else xTo[0], b_xTo[0], sub, dr_X_out[i], tail=False)

        interleave([subgen(0), subgen(1)])
    if XT_out is not None:
        tails(nt - 1)


def phase_B1(c, layer, X_res, dr_Xres, X1, dr_X1, X1T, dr_X1T, bg=None):
    fw, nc = c.fw, c.nc
    nt = c.T // TT
    KC = 10 if layer == 0 else 8
    wname = "w_out_ab" if layer == 0 else "w_out_c"
    wo = sb(c, "wo_%d" % layer, [128, KC, D], BF16)
    b_wo = fw.buf("wo", dma=True)
    load_w_bf16(c, wo, c.din[wname][0], b_wo, col_block=1024)
    gbc = sb(c, "b1g_%d" % layer, [128, D], F32)
    bbc = sb(c, "b1b_%d" % layer, [128, D], F32)
    b_gb = fw.buf("gb1", dma=True)
    load_bc(c, gbc, c.din["ln_mix_g"][layer], b_gb)
    load_bc(c, bbc, c.din["ln_mix_b"][layer], b_gb)
    if layer == 1:
        obc = sb(c, "b1ob", [128, D], F32)
        load_bc(c, obc, c.din["b_out_c"][0], b_gb)
    NB = 2
    O16 = [sb(c, "b1O%d_%d" % (i, layer), [128, KC, TT], BF16) for i in range(NB)]
    b_O16 = [fw.buf("b1O", dma=True) for i in range(NB)]
    xr = [sb(c, "b1xr%d_%d" % (i, layer), [128, 2, D], F32) for i in range(NB)]
    b_xr = [fw.buf("b1xr", dma=True) for i in range(NB)]
    if layer == 0:
        U = [sb(c, "b1U%d" % i, [128, 6, TT], F32) for i in range(NB)]
        Dn = [sb(c, "b1D%d" % i, [128, 6, TT], F32) for i in range(NB)]
        b_U = [fw.buf("b1U", dma=True) for i in range(NB)]
        b_Dn = [fw.buf("b1Dn", dma=True) for i in range(NB)]
        dt = sb(c, "b1dt", [128, 2, TT], F32)
        b_dt = fw.buf()
    yp = [sb(c, "b1yp%d_%d" % (i, layer), [128, D], F32) for i in range(4)]
    b_yp = [fw.buf("b1yp", dma=True) for i in range(4)]
    xTo = [sb(c, "b1xTo%d_%d" % (i, layer), [128, 8, TT], BF16) for i in range(2)]
    b_xTo = [fw.buf("b1xTo", dma=True) for i in range(2)]
    Xrows = X_res.rearrange("(n s p) d -> n p s d", s=2, p=128)
    X1rows = X1.rearrange("(n s p) d -> n s p d", s=2, p=128)

    def load(i):
        s = i % NB
        if layer == 0:
            fw.dma(fw.SP, [(O16[s][:, 0:4, :], c.OTa[i])], reads=[c.dr_OTa[i]], writes=[b_O16[s]], semb=b_O16[s])
            fw.dma(fw.ACT, [(U[s], c.U32.rearrange("(cc p) t -> p cc t", p=128)[:, :, i * TT:(i + 1) * TT])], reads=[c.dr_U32[i]], writes=[b_U[s]], semb=b_U[s])
            fw.dma(fw.ACT, [(Dn[s], c.DEN.rearrange("(cc p) t -> p cc t", p=128)[:, :, i * TT:(i + 1) * TT])], reads=[c.dr_U32[i]], writes=[b_Dn[s]], semb=b_Dn[s])
        else:
            fw.dma(fw.SP, [(O16[s], c.VT[i])], reads=[c.dr_VT[i]], writes=[b_O16[s]], semb=b_O16[s])
        fw.dma(fw.SP, [(xr[s], Xrows[i])], reads=[dr_Xres[i]], writes=[b_xr[s]], semb=b_xr[s])

    def stageA(i):
        s = i % NB
        if layer == 0:
            fw.op(fw.DVE, lambda e: e.tensor_tensor(out=dt, in0=Dn[s][:, 0:2, :], in1=Dn[s][:, 2:4, :], op=ALU.add),
                  reads=[b_Dn[s]], writes=[b_dt])
            fw.op(fw.DVE, lambda e: e.tensor_tensor(out=dt, in0=dt, in1=Dn[s][:, 4:6, :], op=ALU.add),
                  reads=[b_Dn[s], b_dt], writes=[b_dt])
            fw.op(fw.DVE, lambda e: e.reciprocal(out=dt, in_=dt), reads=[b_dt], writes=[b_dt])
            for g in range(3):
                fw.op(fw.DVE, lambda e, g=g: e.tensor_tensor(out=O16[s][:, 4 + 2 * g:6 + 2 * g, :], in0=U[s][:, 2 * g:2 * g + 2, :],
                                                             in1=dt, op=ALU.mult),
                      reads=[b_U[s], b_dt], writes=[b_O16[s]])

        def subgen(sub):
            yi = (i % 2) * 2 + sub
            for half in range(2):
                pj = sub * 2 + half
                py = c.ps[pj]
                fw.mm([(lambda e, k=k, py=py, half=half, sub=sub: e.matmul(
                    py, lhsT=O16[s][:, k, sub * 128:(sub + 1) * 128], rhs=wo[:, k, half * 512:(half + 1) * 512],
                    start=(k == 0), stop=(k == KC - 1))) for k in range(KC)],
                    reads=[b_wo, b_O16[s]], writes=[c.b_ps[pj]])
                yield
                fw.op(fw.DVE, lambda e, py=py, half=half, sub=sub, yi=yi: e.scalar_tensor_tensor(
                    out=yp[yi][:, half * 512:(half + 1) * 512], in0=xr[s][:, sub, half * 512:(half + 1) * 512],
                    scalar=ALPHA, in1=py, op0=ALU.mult, op1=ALU.add),
                    reads=[c.b_ps[pj], b_xr[s]], writes=[b_yp[yi]])
                yield
            if layer == 1:
                fw.op(fw.POOL, lambda e, yi=yi: e.tensor_tensor(out=yp[yi], in0=yp[yi], in1=obc, op=ALU.add),
                      reads=[b_yp[yi], b_gb], writes=[b_yp[yi]])
                yield

        interleave([subgen(0), subgen(1)])

    def stageB(i):
        so = i % 2
        interleave([ln_chain(c, sub, yp[(i % 2) * 2 + sub], b_yp[(i % 2) * 2 + sub], gbc, bbc, b_gb, X1rows[i, sub], xTo[so], b_xTo[so],
                             sub, dr_X1[i]) for sub in range(2)])
        fw.dma(fw.SP, [(X1T[i], xTo[so])], reads=[b_xTo[so]], writes=[dr_X1T[i]], semb=b_xTo[so])

    load(0)
    if nt > 1:
        load(1)
    stageA(0)
    for i in range(nt):
        if i + 2 < nt:
            load(i + 2)
        if bg and i % 2 == 1:
            bg.pop(0)()
        ls = [fw.record(lambda: stageB(i))]
        if i + 1 < nt:
            ls.append(fw.record(lambda: stageA(i + 1)))
        fw.emit_interleaved(ls)


def phase_L1A(c, XT_in, dr_XT_in, VT, dr_VT):
    fw, nc = c.fw, c.nc
    win = sb(c, "cwin", [128, 8, 2048], BF16)
    b_win = fw.buf("cwin", dma=True)
    load_w_bf16(c, win, c.din["w_in_c"][0], b_win)
    b_small = fw.buf("csmall", dma=True)
    bin_ = sb(c, "cbin", [128, 16], F32)
    load_pp(c, bin_, c.din["b_in_c"][0], b_small)
    dwb = sb(c, "cdwb", [128, 8], F32)
    load_pp(c, dwb, c.din["dw_b_c"][0], b_small)
    cg = sb(c, "ccg", [128, 8], F32)
    load_pp(c, cg, c.din["cnorm_g"][0], b_small)
    cb = sb(c, "ccb", [128, 8], F32)
    load_pp(c, cb, c.din["cnorm_b"][0], b_small)
    dw = sb(c, "cdw", [128, 8, 31], F32)
    dwv = c.din["dw_c"][0].rearrange("j (c p) -> p c j", p=128)
    fw.dma(fw.SP, [(dw[:, k, :], dwv[:, k, :]) for k in range(8)], writes=[b_small], semb=b_small, allow_slow_non_contiguous=True)
    CB = 256
    diag = sb(c, "cdiag", [128, 8 * 31, 128], BF16)
    b_diag = fw.buf()
    for k in range(8):
        fw.op(fw.POOL if k % 2 else fw.DVE, lambda e, k=k: e.tensor_tensor(
            out=diag[:, k * 31:(k + 1) * 31, :], in0=c.ident.rearrange("p (o n) -> p o n", o=1).to_broadcast([128, 31, 128]),
            in1=dw[:, k, :].rearrange("p (j o) -> p j o", o=1).to_broadcast([128, 31, 128]), op=ALU.mult),
            reads=[b_small, c.b_const], writes=[b_diag])
    SMAX = max(c.seqs)
    uT = sb(c, "cuT", [128, 8, SMAX + 32], BF16)
    b_uT = fw.buf()
    NB = 2
    xt = [sb(c, "cxt%d" % i, [128, 8, TT], BF16) for i in range(NB)]
    b_xt = [fw.buf("cxt", dma=True) for i in range(NB)]
    sg = [sb(c, "csg%d" % i, [128, TT], F32) for i in range(2)]
    b_sg = [fw.buf() for i in range(2)]
    y32 = sb(c, "cy32", [128, 8, CB], F32)
    b_y32 = [fw.buf() for _ in range(8)]
    y16 = [sb(c, "cy16_%d" % i, [128, CB], BF16) for i in range(2)]
    b_y16 = [fw.buf() for i in range(2)]
    q16 = [sb(c, "cq16_%d" % i, [128, CB], BF16) for i in range(2)]
    b_q16 = [fw.buf() for i in range(2)]
    mean = sb(c, "cmean", [128, CB], F32)
    rstd = sb(c, "crstd", [128, CB], F32)
    b_mr = fw.buf()
    msq = sb(c, "cmsq", [128, CB], F32)
    b_msq = fw.buf()
    tall = sb(c, "ctall", [128, 8, CB], F32)
    b_tall = fw.buf()
    vst = [sb(c, "cvst%d" % i, [128, 8, CB], BF16) for i in range(1)]
    b_vst = [fw.buf("cvst", dma=True) for i in range(1)]
    fw.op(fw.POOL, lambda e: e.memset(uT, 0.0), writes=[b_uT])
    tok0 = 0
    cnt = 0
    vcnt = 0
    for S in c.seqs:
        nt = S // TT
        fw.op(fw.POOL, lambda e, S=S: e.memset(uT[:, :, 15 + S:15 + S + 16], 0.0), writes=[b_uT])
        ti0 = tok0 // TT
        fw.dma(fw.SP, [(xt[0], XT_in[ti0])], reads=[dr_XT_in[ti0]], writes=[b_xt[0]], semb=b_xt[0])
        for i in range(nt):
            s = i % NB
            if i + 1 < nt:
                s1 = (i + 1) % NB
                fw.dma(fw.SP, [(xt[s1], XT_in[ti0 + i + 1])], reads=[dr_XT_in[ti0 + i + 1]], writes=[b_xt[s1]], semb=b_xt[s1])
            for k in range(8):
                pa = cnt % 2
                pg = 2 + cnt % 2
                cnt += 1
                fw.mm([(lambda e, kk=kk, k=k, pa=pa: e.matmul(c.ps[pa][:, 0:TT], lhsT=win[:, kk, k * 128:(k + 1) * 128],
                                                              rhs=xt[s][:, kk, :], start=(kk == 0), stop=(kk == 7)))
                       for kk in range(8)], reads=[b_win, b_xt[s]], writes=[c.b_ps[pa]])
                fw.mm([(lambda e, kk=kk, k=k, pg=pg: e.matmul(c.ps[pg][:, 0:TT], lhsT=win[:, kk, 1024 + k * 128:1024 + (k + 1) * 128],
                                                              rhs=xt[s][:, kk, :], start=(kk == 0), stop=(kk == 7)))
                       for kk in range(8)], reads=[b_win, b_xt[s]], writes=[c.b_ps[pg]])
                si = cnt % 2
                fw.op(fw.ACT, lambda e, pg=pg, k=k, si=si: e.activation(out=sg[si], in_=c.ps[pg][:, 0:TT], func=AF.Sigmoid,
                                                                        bias=bin_[:, 8 + k:9 + k]),
                      reads=[c.b_ps[pg], b_small], writes=[b_sg[si]])
                fw.op(fw.DVE, lambda e, pa=pa, k=k, si=si, i=i: e.scalar_tensor_tensor(
                    out=uT[:, k, 15 + i * TT:15 + (i + 1) * TT], in0=c.ps[pa][:, 0:TT], scalar=bin_[:, k:k + 1], in1=sg[si],
                    op0=ALU.add, op1=ALU.mult), reads=[c.b_ps[pa], b_sg[si], b_small], writes=[b_uT])
        for tb in range(S // CB):
            pend = None
            for k in range(8):
                pc = 4 + cnt % 2
                ds = cnt % 2
                cnt += 1
                fw.mm([(lambda e, j=j, k=k, pc=pc, tb=tb: e.matmul(c.ps[pc][:, 0:CB], lhsT=diag[:, k * 31 + j, :],
                                                                    rhs=uT[:, k, tb * CB + j:tb * CB + j + CB],
                                                                    start=(j == 0), stop=(j == 30))) for j in range(31)],
                      reads=[b_diag, b_uT], writes=[c.b_ps[pc]])
                if pend is not None:
                    pend()
                fw.op(fw.ACT, lambda e, k=k, pc=pc: e.activation(out=y32[:, k, :], in_=c.ps[pc][:, 0:CB], func=AF.Identity,
                                                                 bias=dwb[:, k:k + 1]),
                      reads=[c.b_ps[pc], b_small], writes=[b_y32[k]])
                fw.op(fw.ACT, lambda e, k=k, ds=ds: e.activation(out=y16[ds], in_=y32[:, k, :], func=AF.Copy),
                      reads=[b_y32[k]], writes=[b_y16[ds]])
                fw.op(fw.ACT, lambda e, k=k, ds=ds: e.activation(out=q16[ds], in_=y32[:, k, :], func=AF.Square),
                      reads=[b_y32[k]], writes=[b_q16[ds]])

                def pend(k=k, ds=ds):
                    fw.mm([lambda e: e.matmul(c.ps[6][:, 0:CB], lhsT=c.ones16, rhs=y16[ds], start=(k == 0), stop=(k == 7))],
                          reads=[b_y16[ds], c.b_const], writes=[c.b_ps[6]])
                    fw.mm([lambda e: e.matmul(c.ps[7][:, 0:CB], lhsT=c.ones16, rhs=q16[ds], start=(k == 0), stop=(k == 7))],
                          reads=[b_q16[ds], c.b_const], writes=[c.b_ps[7]])
            pend()
            fw.op(fw.DVE, lambda e: e.tensor_scalar(out=mean, in0=c.ps[6][:, 0:CB], scalar1=1.0 / 1024, scalar2=None, op0=ALU.mult),
                  reads=[c.b_ps[6]], writes=[b_mr])
            fw.op(fw.DVE, lambda e: e.tensor_tensor(out=msq, in0=mean, in1=mean, op=ALU.mult), reads=[b_mr], writes=[b_msq])
            fw.op(fw.DVE, lambda e: e.scalar_tensor_tensor(out=rstd, in0=c.ps[7][:, 0:CB], scalar=1.0 / 1024, in1=msq,
                                                           op0=ALU.mult, op1=ALU.subtract),
                  reads=[c.b_ps[7], b_msq], writes=[b_mr])
            fw.op(fw.ACT, lambda e: e.activation(out=rstd, in_=rstd, func=AF.Ln, bias=c.lneps_col[:, 0:1]), reads=[b_mr, c.b_const], writes=[b_mr])
            fw.op(fw.ACT, lambda e: e.activation(out=rstd, in_=rstd, func=AF.Exp, scale=-0.5), reads=[b_mr], writes=[b_mr])
            bc8 = lambda ap: ap.rearrange("p (o t) -> p o t", o=1).to_broadcast([128, 8, CB])
            fw.op(fw.DVE, lambda e: e.tensor_tensor(out=tall, in0=y32, in1=bc8(mean), op=ALU.subtract), reads=b_y32 + [b_mr], writes=[b_tall])
            fw.op(fw.POOL, lambda e: e.tensor_tensor(out=tall, in0=tall, in1=bc8(rstd), op=ALU.mult), reads=[b_tall, b_mr], writes=[b_tall])
            for k in range(8):
                fw.op(fw.ACT, lambda e, k=k: e.activation(out=vst[0][:, k, :], in_=tall[:, k, :], func=AF.Silu,
                                                          scale=cg[:, k:k + 1], bias=cb[:, k:k + 1]),
                      reads=[b_tall, b_small], writes=[b_vst[0]])
            t_i = (tok0 + tb * CB) // TT
            fw.dma(fw.SP, [(VT[t_i], vst[0])], reads=[b_vst[0]], writes=[dr_VT[t_i]], semb=b_vst[0])
        tok0 += S


WNAMES = [("rel_bias", [32, 12]), ("hgrn_lb", [2, 3, 512]), ("w_in_ab", [1, 1024, 4864]), ("hgrn_norm", [1, 512]),
          ("w_out_ab", [1, 1280, 1024]), ("w_in_c", [1, 1024, 2048]), ("b_in_c", [1, 2048]), ("dw_c", [1, 31, 1024]),
          ("dw_b_c", [1, 1024]), ("cnorm_g", [1, 1024]), ("cnorm_b", [1, 1024]), ("w_out_c", [1, 1024, 1024]),
          ("b_out_c", [1, 1024]), ("ln_mix_g", [2, 1024]), ("ln_mix_b", [2, 1024]), ("mlp_w1", [2, 1024, 4096]),
          ("mlp_w2", [2, 4096, 1024]), ("ln_ffn_g", [2, 1024]), ("ln_ffn_b", [2, 1024])]


def build(seqs, phases=("L0A", "L0B1", "L0B2", "L1A", "L1B1", "L1B2"), ext=()):
    from contextlib import ExitStack
    nc = bass.Bass("TRN2", target_bir_lowering=False)
    c = Ctx()
    c.nc = nc
    c.seqs = list(seqs)
    c.T = T = sum(seqs)
    c.uid = 0
    nt = T // TT
    c.din = {}
    c.din["x"] = nc.dram_tensor("x", [T, D], F32, kind="ExternalInput").ap()
    for n, shp in WNAMES:
        c.din[n] = nc.dram_tensor(n, shp, F32, kind="ExternalInput").ap()
    c.din["oh"] = nc.dram_tensor("oh", [3, 33, 384], F32, kind="ExternalInput").ap()
    y = nc.dram_tensor("y", [T, D], F32, kind="ExternalOutput").ap()

    def scratch(name, shape, dt):
        kind = "Internal"
        if ("in:" + name) in ext:
            kind = "ExternalInput"
        if ("out:" + name) in ext:
            kind = "ExternalOutput"
        return nc.dram_tensor(name, shape, dt, kind=kind).ap()

    c.OTa = scratch("OTa", [nt, 128, 4, TT], BF16)
    c.U32 = scratch("U32", [768, T], F32)
    c.DEN = scratch("DEN", [768, T], F32)
    c.TV = scratch("TV", [3, 4, 384], F32)
    c.VT = scratch("VT", [nt, 128, 8, TT], BF16)
    X1 = scratch("X1", [T, D], F32)
    X1T = scratch("X1T", [nt, 128, 8, TT], BF16)
    X2 = scratch("X2", [T, D], F32)
    X2T = scratch("X2T", [nt, 128, 8, TT], BF16)
    c.fw = fw = FW(nc)
    mk = lambda: [fw.buf(multi=True) for _ in range(nt)]
    c.dr_OTa, c.dr_U32, c.dr_VT = mk(), mk(), mk()
    dr_x, dr_X1, dr_X1T, dr_X2, dr_X2T, dr_y = mk(), mk(), mk(), mk(), mk(), mk()
    with ExitStack() as es_g:
        c.es = es_g
        setup_consts(c)
        X1r = X1.rearrange("(n s p) d -> n s p d", s=2, p=128)
        w1s = {}
        for ph in phases:
            layer = 0 if ph.startswith("L0") else 1
            if ph.endswith("B1") and (ph[:2] + "B2") in phases:
                es_l = ExitStack()
                c.es = es_l
                w1 = sb(c, "w1_%d" % layer, [128, 8, DFF], BF16)
                b_w1 = fw.buf()
                w1s[layer] = (w1, b_w1, w_chunks(c, w1, c.din["mlp_w1"][layer], b_w1), es_l)
            with ExitStack() as es:
                c.es = es
                if ph.endswith("B2") and layer not in w1s:
                    w1 = sb(c, "w1_%d" % layer, [128, 8, DFF], BF16)
                    b_w1 = fw.buf()
                    w1s[layer] = (w1, b_w1, w_chunks(c, w1, c.din["mlp_w1"][layer], b_w1), None)
                if ph == "L0A":
                    phase_L0A(c)
                elif ph == "L0B1":
                    phase_B1(c, 0, c.din["x"], dr_x, X1, dr_X1, X1T, dr_X1T, bg=w1s[0][2] if 0 in w1s else None)
                elif ph == "L0B2":
                    X2r = X2.rearrange("(n p) d -> n p d", p=128)
                    phase_B2(c, 0, X1T, dr_X1T, X1, dr_X1, lambda t0: X2r[t0 // 128], dr_X2, X2T, dr_X2T, *w1s[0][:3])
                elif ph == "L1A":
                    phase_L1A(c, X2T, dr_X2T, c.VT, c.dr_VT)
                elif ph == "L1B1":
                    phase_B1(c, 1, X2, dr_X2, X1, dr_X1, X1T, dr_X1T, bg=w1s[1][2] if 1 in w1s else None)
                elif ph == "L1B2":
                    yr = y.rearrange("(n p) d -> n p d", p=128)
                    phase_B2(c, 1, X1T, dr_X1T, X1, dr_X1, lambda t0: yr[t0 // 128], dr_y, None, dr_y, *w1s[1][:3])
                fw.end_phase()
            if ph.endswith("B2") and w1s[layer][3] is not None:
                w1s[layer][3].close()
        fw.barrier()
    return nc


SEG = 512
GROUP_R = (1, 4, 16)
SKIP_HGRN = False
SKIP_ATT = False


def t5_buckets_np(rel):
    half = 16
    max_exact = 8
    n = np.abs(rel)
    large = max_exact + (np.log(np.maximum(n, 1) / max_exact) / np.log(1024 / max_exact) * (half - max_exact)).astype(np.int32)
    large = np.minimum(large, half - 1)
    return (np.where(rel > 0, half, 0) + np.where(n < max_exact, n, large)).astype(np.int32)


def onehot_const():
    oh = np.zeros((3, 33, 384), np.float32)
    for g, r in enumerate(GROUP_R):
        rel = np.arange(383) - 191
        b = t5_buckets_np(rel * r)
        b = np.where(np.abs(rel) <= 64, b, 32)
        oh[g, b, np.arange(383)] = 1.0
    return oh


def l0a_setup(c):
    fw, nc = c.fw, c.nc
    b_s = fw.buf("l0s", dma=True)
    c.b_l0s = b_s
    lbraw = sb(c, "lbraw", [128, 8, 3], F32)
    src = c.din["hgrn_lb"].rearrange("a l (h p) -> p a h l", p=128)
    fw.dma(fw.SP, [(lbraw[:, a * 4 + h, :], src[:, a, h, :]) for a in range(2) for h in range(4)], writes=[b_s], semb=b_s,
           allow_slow_non_contiguous=True)
    fw.op(fw.ACT, lambda e: e.activation(out=lbraw, in_=lbraw, func=AF.Exp), reads=[b_s], writes=[b_s])
    lsum = sb(c, "lsum", [128, 8], F32)
    fw.op(fw.DVE, lambda e: e.tensor_reduce(out=lsum, in_=lbraw, axis=AX.X, op=ALU.add), reads=[b_s], writes=[b_s])
    fw.op(fw.DVE, lambda e: e.reciprocal(out=lsum, in_=lsum), reads=[b_s], writes=[b_s])
    c.lb = sb(c, "lb", [128, 8], F32)
    c.oml = sb(c, "oml", [128, 8], F32)
    c.noml = sb(c, "noml", [128, 8], F32)
    fw.op(fw.DVE, lambda e: e.tensor_tensor(out=c.lb, in0=lbraw[:, :, 0], in1=lsum, op=ALU.mult), reads=[b_s], writes=[b_s])
    fw.op(fw.DVE, lambda e: e.tensor_scalar(out=c.oml, in0=c.lb, scalar1=-1.0, scalar2=1.0, op0=ALU.mult, op1=ALU.add),
          reads=[b_s], writes=[b_s])
    fw.op(fw.DVE, lambda e: e.tensor_scalar(out=c.noml, in0=c.lb, scalar1=-1.0, scalar2=None, op0=ALU.add), reads=[b_s], writes=[b_s])
    c.lnoml = sb(c, "lnoml", [128, 8], F32)
    fw.op(fw.ACT, lambda e: e.activation(out=c.lnoml, in_=c.oml, func=AF.Ln), reads=[b_s], writes=[b_s])
    c.one_col = sb(c, "one_col", [128, 1], F32)
    fw.op(fw.DVE, lambda e: e.memset(c.one_col, 1.0), writes=[b_s])
    c.eps_col = sb(c, "eps_col", [128, 1], F32)
    fw.op(fw.DVE, lambda e: e.memset(c.eps_col, RMS_EPS), writes=[b_s])
    c.gnorm = sb(c, "gnorm", [128, 4], F32)
    load_pp(c, c.gnorm, c.din["hgrn_norm"][0], b_s)
    c.maskF = sb(c, "maskF", [128, 64], F32)
    c.maskB = sb(c, "maskB", [128, 64], F32)
    fw.op(fw.POOL, lambda e: e.memset(c.maskF, 1.0), writes=[b_s])
    fw.op(fw.POOL, lambda e: e.memset(c.maskB, 1.0), writes=[b_s])
    for pb in (0, 64):
        fw.op(fw.POOL, lambda e, pb=pb: e.affine_select(out=c.maskF[pb:pb + 64, :], in_=c.maskF[pb:pb + 64, :], pattern=[[1, 64]],
                                                       compare_op=ALU.is_ge, fill=0.0, base=0, channel_multiplier=-1),
              reads=[b_s], writes=[b_s])
        fw.op(fw.POOL, lambda e, pb=pb: e.affine_select(out=c.maskB[pb:pb + 64, :], in_=c.maskB[pb:pb + 64, :], pattern=[[-1, 64]],
                                                       compare_op=ALU.is_ge, fill=0.0, base=0, channel_multiplier=1),
              reads=[b_s], writes=[b_s])
    c.rmask = sb(c, "rmask", [128, SEG], F32)
    fw.op(fw.POOL, lambda e: e.memset(c.rmask, 1.0), writes=[b_s])
    fw.op(fw.POOL, lambda e: e.affine_select(out=c.rmask, in_=c.rmask, pattern=[[0, SEG // 64], [1, 64]], compare_op=ALU.not_equal,
                                             fill=0.0, base=0, channel_multiplier=0), reads=[b_s], writes=[b_s])
    c.tabs = sb(c, "tabs", [128, 24, 128], F32)
    c.b_tabs = fw.buf()
    with ExitStackLocal(c) as _:
        J = sb(c, "J", [128, 128], F32)
        fw.op(fw.POOL, lambda e: e.memset(J, 0.0), writes=[b_s])
        fw.op(fw.POOL, lambda e: e.affine_select(out=J, in_=J, pattern=[[1, 128]], compare_op=ALU.not_equal, fill=1.0, base=-127,
                                                 channel_multiplier=1), reads=[b_s], writes=[b_s])
        rba = sb(c, "rba", [33, 12], F32)
        fw.dma(fw.SP, [(rba[0:32, :], c.din["rel_bias"])], writes=[b_s], semb=b_s)
        fw.op(fw.POOL, lambda e: e.memset(rba[32:33, :], NEG), writes=[b_s])
        ohs = sb(c, "ohs", [33, 3, 384], F32)
        fw.dma(fw.SP, [(ohs, c.din["oh"].rearrange("g b j -> b g j"))], writes=[b_s], semb=b_s)
        tvs = sb(c, "tvs", [4, 3, 384], F32)
        b_tv = fw.buf("tv", dma=True)
        for g in range(3):
            fw.mm([lambda e, g=g: e.matmul(c.ps[0][0:4, 0:384], lhsT=rba[:, 4 * g:4 * g + 4], rhs=ohs[:, g, :], start=True, stop=True)],
                  reads=[b_s], writes=[c.b_ps[0]])
            fw.op(fw.DVE, lambda e, g=g: e.tensor_copy(out=tvs[:, g, :], in_=c.ps[0][0:4, 0:384]), reads=[c.b_ps[0]], writes=[b_tv])
        b_tvd = fw.buf(multi=True)
        fw.dma(fw.SP, [(c.TV.rearrange("g h j -> h g j"), tvs)], reads=[b_tv], writes=[b_tvd], semb=b_tv)
        hk = [sb(c, "hk%d" % i, [128, 128], F32) for i in range(2)]
        b_hk = [fw.buf("hk", dma=True) for i in range(2)]
        n = 0
        for g in range(3):
            for hh in range(4):
                for ty in range(2):
                    s = n % 2
                    off = (g * 4 + hh) * 384 + (0 if ty == 0 else 128)
                    src = bass.AP(c.TV.tensor, c.TV.offset + off, [[1, 128], [1, 128]])
                    fw.dma(fw.SP, [(hk[s], src)], reads=[b_tvd], writes=[b_hk[s]], semb=b_hk[s])
                    pi = n % 2
                    fw.mm([lambda e, s=s, pi=pi: e.matmul(c.ps[pi][:, 0:128], lhsT=hk[s], rhs=J, start=True, stop=True)],
                          reads=[b_hk[s], b_s], writes=[c.b_ps[pi]])
                    fw.op(fw.DVE, lambda e, n=n, pi=pi: e.tensor_copy(out=c.tabs[:, n, :], in_=c.ps[pi][:, 0:128]),
                          reads=[c.b_ps[pi]], writes=[c.b_tabs])
                    n += 1
        fw.barrier()


def cbuf(c):
    rp = getattr(c, "replay", None)
    if rp is None:
        return c.fw.buf()
    i = rp["i_buf"]
    rp["i_buf"] += 1
    if i < len(rp["buf"]):
        return rp["buf"][i]
    b_ = c.fw.buf()
    rp["buf"].append(b_)
    return b_


class ExitStackLocal:
    def __init__(self, c):
        from contextlib import ExitStack
        self.c = c
        self.es = ExitStack()

    def __enter__(self):
        self.prev = self.c.es
        self.es.__enter__()
        self.c.es = self.es
        return self

    def __exit__(self, *a):
        self.c.es = self.prev
        return self.es.__exit__(*a)


def hgrn_head(c, h, S, tok0, xT, b_xT, wh, b_wh, prefetch):
    fw, nc = c.fw, c.nc
    nseg = S // SEG
    ntile = S // 128
    Vtok = sb(c, "Vtok", [128, ntile, 128], BF16)
    qs = sb(c, "qs", [128, S], BF16)
    G = sb(c, "G", [128, S], BF16)
    OF = sb(c, "OF", [128, ntile, 128], F32)
    b_V, b_qs, b_G, b_OF = cbuf(c), cbuf(c), cbuf(c), cbuf(c)
    gt = sb(c, "gt", [128, SEG], F32)
    b_gt = cbuf(c)
    pc = 0
    for j in range(ntile):
        pi = pc % 2
        pc += 1
        fw.mm([(lambda e, k=k, j=j, pi=pi: e.matmul(c.ps[pi][:, 0:128], lhsT=xT[:, k, j * 128:(j + 1) * 128], rhs=wh[:, k, 1, :],
                                                    start=(k == 0), stop=(k == 7))) for k in range(8)],
              reads=[b_xT, b_wh], writes=[c.b_ps[pi]])
        fw.op(fw.ACT, lambda e, j=j, pi=pi: e.activation(out=Vtok[:, j, :], in_=c.ps[pi][:, 0:128], func=AF.Copy),
              reads=[c.b_ps[pi]], writes=[b_V])
    for sg in range(nseg):
        cs = slice(sg * SEG, (sg + 1) * SEG)
        pi = pc % 2
        pc += 1
        fw.mm([(lambda e, k=k, pi=pi: e.matmul(c.ps[pi], lhsT=wh[:, k, 0, :], rhs=xT[:, k, cs], start=(k == 0), stop=(k == 7)))
               for k in range(8)], reads=[b_xT, b_wh], writes=[c.b_ps[pi]])
        fw.op(fw.ACT, lambda e, pi=pi: e.activation(out=qs[:, cs], in_=c.ps[pi], func=AF.Silu), reads=[c.b_ps[pi]], writes=[b_qs])
        pi = pc % 2
        pc += 1
        fw.mm([(lambda e, k=k, pi=pi: e.matmul(c.ps[pi], lhsT=wh[:, k, 4, :], rhs=xT[:, k, cs], start=(k == 0), stop=(k == 7)))
               for k in range(8)], reads=[b_xT, b_wh], writes=[c.b_ps[pi]])
        fw.op(fw.ACT, lambda e, pi=pi: e.activation(out=gt, in_=c.ps[pi], func=AF.Silu), reads=[c.b_ps[pi]], writes=[b_gt])
        fw.op(fw.DVE, lambda e: e.tensor_scalar(out=G[:, cs], in0=gt, scalar1=c.gnorm[:, h:h + 1], scalar2=None, op0=ALU.mult),
              reads=[b_gt, c.b_l0s], writes=[b_G])
    prefetch()
    SH = 256
    NJ = SH // 128
    NC = SH // 64
    nsg = S // SH
    v3 = lambda ap: ap.rearrange("p (c t) -> p c t", t=64)
    c1 = lambda ap: ap.rearrange("p (c o) -> p c o", o=1)
    bC = lambda d: 3 * d
    bA = lambda d: 3 * d + 1
    bB = lambda d: 3 * d + 2
    b_OFs = [cbuf(c) for _ in range(nsg)]
    pt = c.ps_tr

    def mkset(d):
        B = Ctx()
        for nm in ("sig", "logf", "bl", "Pb", "X", "Ep", "Em"):
            setattr(B, nm, sb(c, nm + str(d), [128, SH], F32))
            setattr(B, "b_" + nm, cbuf(c))
        B.qt = [sb(c, "qt%d_%d" % (d, i), [128, SH], BF16) for i in range(2)]
        B.kt = [sb(c, "kt%d_%d" % (d, i), [128, SH], BF16) for i in range(2)]
        B.Ktok = [sb(c, "Ktok%d_%d" % (d, i), [128, NJ, 128], BF16) for i in range(2)]
        B.A16 = [sb(c, "A16_%d_%d" % (d, i), [128, NJ, 64], BF16) for i in range(2)]
        B.Tsb = [sb(c, "Tsb%d_%d" % (d, i), [128, NC, 128], F32) for i in range(2)]
        B.sm = [sb(c, "sm%d_%d" % (d, i), [128, 40], F32) for i in range(2)]
        for nm in ("qt", "kt", "Ktok", "A16", "Tsb", "sm"):
            setattr(B, "b_" + nm, [cbuf(c), cbuf(c)])
        B.car = sb(c, "car%d" % d, [128, 2], F32)
        B.Ubuf = sb(c, "Ubuf%d" % d, [128, NC + 1, 128], F32)
        B.S16 = sb(c, "S16_%d" % d, [128, NC, 128], BF16)
        B.osum = sb(c, "osum%d" % d, [128, NJ, 128], F32)
        B.sq = sb(c, "sq%d" % d, [128, NJ, 128], F32)
        B.on16 = sb(c, "on16_%d" % d, [128, NJ, 128], BF16)
        B.ss = sb(c, "ss%d" % d, [128, 4 * NJ], F32)
        B.ost = [sb(c, "ost%d_%d" % (d, i), [128, SH], BF16) for i in range(2)]
        for nm in ("car", "U", "S16", "osum", "sq", "on", "ss"):
            setattr(B, "b_" + nm, cbuf(c))
        B.b_ost = [c.dmabuf("ost%d_%d" % (d, i)) for i in range(2)]
        B.ocnt = 0
        return B

    Bs = [mkset(0), mkset(1)]

    def prep(d, sg, pp, first):
        B = Bs[d]
        sgn = 1.0 if d == 0 else -1.0
        mask = c.maskF if d == 0 else c.maskB
        lbi = d * 4 + h
        cs = slice(sg * SH, (sg + 1) * SH)
        S_ = B.sm[pp]
        pz = c.ps[bC(d)][:, 0:SH]
        fw.mm([(lambda e, k=k: e.matmul(pz, lhsT=wh[:, k, 2 + d, :], rhs=xT[:, k, cs], start=(k == 0), stop=(k == 7)))
               for k in range(8)], reads=[b_xT, b_wh], writes=[c.b_ps[bC(d)]])
        fw.op(fw.ACT, lambda e: e.activation(out=B.sig, in_=pz, func=AF.Exp), reads=[c.b_ps[bC(d)]], writes=[B.b_sig])
        fw.op(fw.ACT, lambda e: e.activation(out=B.Ep, in_=B.sig, func=AF.Ln, bias=c.one_col[:, 0:1]), reads=[B.b_sig, c.b_l0s], writes=[B.b_Ep])
        fw.op(fw.ACT, lambda e: e.activation(out=B.sig, in_=B.Ep, func=AF.Exp, scale=-1.0), reads=[B.b_Ep], writes=[B.b_sig])
        fw.op(fw.ACT, lambda e: e.activation(out=B.logf, in_=B.sig, func=AF.Ln, scale=c.noml[:, lbi:lbi + 1], bias=c.one_col[:, 0:1]),
              reads=[B.b_sig, c.b_l0s], writes=[B.b_logf])
        fw.op(fw.DVE, lambda e: e.tensor_tensor_scan(out=B.bl, data0=c.rmask[:, 0:SH], data1=B.logf, initial=0.0, op0=ALU.mult, op1=ALU.add),
              reads=[B.b_logf, c.b_l0s], writes=[B.b_bl])
        if d == 0:
            P, b_P = B.bl, B.b_bl
        else:
            fw.op(fw.DVE, lambda e: e.tensor_tensor(out=B.Pb, in0=B.bl, in1=B.logf, op=ALU.subtract), reads=[B.b_bl, B.b_logf], writes=[B.b_Pb])
            P, b_P = B.Pb, B.b_Pb
        P3, bl3 = v3(P), v3(B.bl)
        fw.op(fw.DVE, lambda e: e.tensor_tensor(out=v3(B.X), in0=P3, in1=P3[:, :, 32:33].to_broadcast([128, NC, 64]), op=ALU.subtract),
              reads=[b_P], writes=[B.b_X])
        fw.op(fw.DVE, lambda e: e.tensor_copy(out=c1(S_[:, 0:NC]), in_=P3[:, :, 32:33]), reads=[b_P], writes=[B.b_sm[pp]])
        fw.op(fw.DVE, lambda e: e.tensor_tensor(out=c1(S_[:, 8:8 + NC]), in0=bl3[:, :, 63:64], in1=c1(S_[:, 0:NC]), op=ALU.subtract),
              reads=[B.b_bl, B.b_sm[pp]], writes=[B.b_sm[pp]])
        if first:
            fw.op(fw.DVE, lambda e: e.memset(B.car, 0.0), writes=[B.b_car])
        if d == 0:
            fw.op(fw.DVE, lambda e: e.tensor_tensor(out=S_[:, 17:16 + NC], in0=S_[:, 8:7 + NC], in1=S_[:, 1:NC], op=ALU.add),
                  reads=[B.b_sm[pp]], writes=[B.b_sm[pp]])
            fw.op(fw.DVE, lambda e: e.tensor_tensor(out=S_[:, 16:17], in0=B.car[:, 0:1], in1=S_[:, 0:1], op=ALU.add),
                  reads=[B.b_sm[pp], B.b_car], writes=[B.b_sm[pp]])
            fw.op(fw.DVE, lambda e: e.tensor_copy(out=B.car[:, 0:1], in_=S_[:, 7 + NC:8 + NC]), reads=[B.b_sm[pp]], writes=[B.b_car])
        else:
            fw.op(fw.DVE, lambda e: e.tensor_tensor(out=S_[:, 16:15 + NC], in0=S_[:, 8:7 + NC], in1=S_[:, 1:NC], op=ALU.add),
                  reads=[B.b_sm[pp]], writes=[B.b_sm[pp]])
            fw.op(fw.DVE, lambda e: e.tensor_tensor(out=S_[:, 15 + NC:16 + NC], in0=S_[:, 7 + NC:8 + NC], in1=B.car[:, 0:1], op=ALU.add),
                  reads=[B.b_sm[pp], B.b_car], writes=[B.b_sm[pp]])
            fw.op(fw.DVE, lambda e: e.tensor_copy(out=B.car[:, 0:1], in_=S_[:, 0:1]), reads=[B.b_sm[pp]], writes=[B.b_car])
        fw.op(fw.ACT, lambda e: e.activation(out=S_[:, 24:24 + NC], in_=S_[:, 16:16 + NC], func=AF.Exp), reads=[B.b_sm[pp]], writes=[B.b_sm[pp]])
        fw.op(fw.ACT, lambda e: e.activation(out=B.Ep, in_=B.X, func=AF.Exp, scale=sgn, bias=c.lnoml[:, lbi:lbi + 1]),
              reads=[B.b_X, c.b_l0s], writes=[B.b_Ep])
        fw.op(fw.ACT, lambda e: e.activation(out=B.Em, in_=B.X, func=AF.Exp, scale=-sgn), reads=[B.b_X], writes=[B.b_Em])
        fw.op(fw.DVE, lambda e: e.tensor_tensor(out=B.qt[pp], in0=qs[:, cs], in1=B.Ep, op=ALU.mult), reads=[b_qs, B.b_Ep], writes=[B.b_qt[pp]])
        fw.op(fw.POOL, lambda e: e.tensor_tensor(out=B.kt[pp], in0=B.sig, in1=B.Em, op=ALU.mult), reads=[B.b_sig, B.b_Em], writes=[B.b_kt[pp]])
        ptk = c.ps[bC(d)].bitcast(BF16)[:, 768:1024]
        fw.mm([(lambda e, j=j: e.transpose(out=ptk[:, j * 128:(j + 1) * 128], in_=B.kt[pp][:, j * 128:(j + 1) * 128], identity=c.ident))
               for j in range(NJ)], reads=[B.b_kt[pp], c.b_const], writes=[c.b_ps[bC(d)]])
        fw.op(fw.DVE, lambda e: e.tensor_copy(out=B.Ktok[pp], in_=ptk.rearrange("p (j t) -> p j t", t=128)),
              reads=[c.b_ps[bC(d)]], writes=[B.b_Ktok[pp]])
        pa = c.ps[bC(d)][:, 256:384]
        fw.mm([(lambda e, ci=ci: e.matmul(pa[(ci % 2) * 64:(ci % 2) * 64 + 64, (ci // 2) * 64:(ci // 2) * 64 + 64],
                                          lhsT=B.kt[pp][:, ci * 64:(ci + 1) * 64], rhs=B.qt[pp][:, ci * 64:(ci + 1) * 64], start=True, stop=True))
               for ci in range(NC)], reads=[B.b_kt[pp], B.b_qt[pp]], writes=[c.b_ps[bC(d)]])
        fw.op(fw.DVE, lambda e: e.tensor_tensor(out=B.A16[pp], in0=pa[:, 0:NJ * 64].rearrange("p (j t) -> p j t", t=64),
                                                in1=mask.rearrange("p (o t) -> p o t", o=1).to_broadcast([128, NJ, 64]), op=ALU.mult),
              reads=[c.b_ps[bC(d)], c.b_l0s], writes=[B.b_A16[pp]])
        fw.mm([(lambda e, ci=ci: e.matmul(c.ps[bA(d) + ci % 2][:, (ci // 2) * 128:(ci // 2 + 1) * 128],
                                          lhsT=B.Ktok[pp][(ci % 2) * 64:(ci % 2) * 64 + 64, ci // 2, :],
                                          rhs=Vtok[(ci % 2) * 64:(ci % 2) * 64 + 64, sg * NJ + ci // 2, :], start=True, stop=True))
               for ci in range(NC)], reads=[B.b_Ktok[pp], b_V], writes=[c.b_ps[bA(d)], c.b_ps[bB(d)]])
        fw.op(fw.ACT, lambda e: e.activation(out=B.Tsb[pp][:, 0:NC:2, :], in_=c.ps[bA(d)][:, 0:SH].rearrange("p (j t) -> p j t", t=128), func=AF.Copy),
              reads=[c.b_ps[bA(d)]], writes=[B.b_Tsb[pp]])
        fw.op(fw.ACT, lambda e: e.activation(out=B.Tsb[pp][:, 1:NC:2, :], in_=c.ps[bB(d)][:, 0:SH].rearrange("p (j t) -> p j t", t=128), func=AF.Copy),
              reads=[c.b_ps[bB(d)]], writes=[B.b_Tsb[pp]])

    def chain(d, sg, pp, first, combine):
        B = Bs[d]
        S_ = B.sm[pp]
        Ubuf = B.Ubuf
        if first:
            fw.op(fw.DVE, lambda e: e.memset(Ubuf[:, 0 if d == 0 else NC, :], 0.0), writes=[B.b_U])
        elif d == 0:
            fw.op(fw.DVE, lambda e: e.tensor_copy(out=Ubuf[:, 0, :], in_=Ubuf[:, NC, :]), reads=[B.b_U], writes=[B.b_U])
        else:
            fw.op(fw.DVE, lambda e: e.tensor_copy(out=Ubuf[:, NC, :], in_=Ubuf[:, 0, :]), reads=[B.b_U], writes=[B.b_U])
        order = range(NC) if d == 0 else range(NC - 1, -1, -1)
        for ci in order:
            src, dst = (ci, ci + 1) if d == 0 else (ci + 1, ci)
            fw.op(fw.DVE, lambda e, ci=ci, src=src, dst=dst: e.scalar_tensor_tensor(
                out=Ubuf[:, dst, :], in0=Ubuf[:, src, :], scalar=S_[:, 24 + ci:25 + ci], in1=B.Tsb[pp][:, ci, :], op0=ALU.mult, op1=ALU.add),
                reads=[B.b_U, B.b_sm[pp], B.b_Tsb[pp]], writes=[B.b_U])
        a0 = 0 if d == 0 else 1
        fw.op(fw.POOL, lambda e: e.tensor_tensor(out=B.S16, in0=Ubuf[:, a0:a0 + NC, :], in1=c1(S_[:, 24:24 + NC]).to_broadcast([128, NC, 128]),
                                                 op=ALU.mult), reads=[B.b_U, B.b_sm[pp]], writes=[B.b_S16])
        fns = []
        for ci in order:
            pb, j = (ci % 2) * 64, ci // 2
            po = c.ps[bA(d) + ci % 2][pb:pb + 64, 256 + j * 128:256 + (j + 1) * 128]
            fns.append(lambda e, po=po, pb=pb, j=j: e.matmul(po, lhsT=B.A16[pp][pb:pb + 64, j, :], rhs=Vtok[pb:pb + 64, sg * NJ + j, :],
                                                            start=True, stop=False))
            fns.append(lambda e, po=po, ci=ci: e.matmul(po, lhsT=B.qt[pp][:, ci * 64:(ci + 1) * 64], rhs=B.S16[:, ci, :], start=False, stop=True))
        fw.mm(fns, reads=[B.b_A16[pp], b_V, B.b_qt[pp], B.b_S16], writes=[c.b_ps[bA(d)], c.b_ps[bB(d)]])
        tl = slice(sg * NJ, (sg + 1) * NJ)
        o3 = lambda bank, lo: c.ps[bank][lo:lo + 64, 256:512].rearrange("p (j t) -> p j t", t=128)
        if not combine:
            fw.op(fw.ACT, lambda e: e.activation(out=OF[0:64, tl, :], in_=o3(bA(d), 0), func=AF.Copy), reads=[c.b_ps[bA(d)]], writes=[b_OFs[sg]])
            fw.op(fw.ACT, lambda e: e.activation(out=OF[64:128, tl, :], in_=o3(bB(d), 64), func=AF.Copy), reads=[c.b_ps[bB(d)]], writes=[b_OFs[sg]])
            return
        fw.op(fw.DVE, lambda e: e.tensor_tensor(out=B.osum[0:64], in0=OF[0:64, tl, :], in1=o3(bA(d), 0), op=ALU.add),
              reads=[c.b_ps[bA(d)], b_OFs[sg]], writes=[B.b_osum])
        fw.op(fw.DVE, lambda e: e.tensor_tensor(out=B.osum[64:128], in0=OF[64:128, tl, :], in1=o3(bB(d), 64), op=ALU.add),
              reads=[c.b_ps[bB(d)], b_OFs[sg]], writes=[B.b_osum])
        fw.op(fw.ACT, lambda e: e.activation(out=B.sq, in_=B.osum, func=AF.Square), reads=[B.b_osum], writes=[B.b_sq])
        ss = B.ss
        fw.op(fw.DVE, lambda e: e.tensor_reduce(out=ss[:, 0:NJ], in_=B.sq, axis=AX.X, op=ALU.add), reads=[B.b_sq], writes=[B.b_ss])
        fw.op(fw.ACT, lambda e: e.activation(out=ss[:, NJ:2 * NJ], in_=ss[:, 0:NJ], func=AF.Ln, scale=1.0 / 128, bias=c.eps_col[:, 0:1]),
              reads=[B.b_ss, c.b_l0s], writes=[B.b_ss])
        fw.op(fw.ACT, lambda e: e.activation(out=ss[:, 3 * NJ:4 * NJ], in_=ss[:, NJ:2 * NJ], func=AF.Exp, scale=-0.5), reads=[B.b_ss], writes=[B.b_ss])
        fw.op(fw.POOL, lambda e: e.tensor_tensor(out=B.on16, in0=B.osum, in1=c1(ss[:, 3 * NJ:4 * NJ]).to_broadcast([128, NJ, 128]), op=ALU.mult),
              reads=[B.b_osum, B.b_ss], writes=[B.b_on])
        pto = c.ps[6 + d].bitcast(BF16)[:, 0:SH]
        fw.mm([(lambda e, j=j: e.transpose(out=pto[:, j * 128:(j + 1) * 128], in_=B.on16[:, j, :], identity=c.ident))
               for j in range(NJ)], reads=[B.b_on, c.b_const], writes=[c.b_ps[6 + d]])
        os_ = B.ocnt % 2
        B.ocnt += 1
        fw.op(fw.DVE, lambda e: e.tensor_tensor(out=B.ost[os_], in0=pto, in1=G[:, sg * SH:(sg + 1) * SH], op=ALU.mult),
              reads=[c.b_ps[6 + d], b_G], writes=[B.b_ost[os_]])
        ti = (tok0 + sg * SH) // TT
        fw.dma(fw.SP, [(c.OTa[ti][:, h, :], B.ost[os_])], reads=[B.b_ost[os_]], writes=[c.dr_OTa[ti]], semb=B.b_ost[os_])

    seg_of = lambda d, i: i if d == 0 else nsg - 1 - i
    prep(0, seg_of(0, 0), 0, True)
    prep(1, seg_of(1, 0), 0, True)
    for i in range(nsg):
        ls = []
        if i + 1 < nsg:
            for d in range(2):
                ls.append(fw.record(lambda d=d: prep(d, seg_of(d, i + 1), (i + 1) % 2, False)))
        for d in range(2):
            ls.append(fw.record(lambda d=d: chain(d, seg_of(d, i), i % 2, i == 0, i >= nsg // 2)))
        fw.emit_interleaved(ls)


def attn_group(c, g, S, tok0, xT, b_xT, wa, b_wa, prefetch):
    fw, nc = c.fw, c.nc
    r = GROUP_R[g]
    L = S // r
    nkt = L // 128
    QT = sb(c, "QT", [128, 2, S], BF16)
    KT = sb(c, "KT", [128, 2, S], BF16)
    Vt = sb(c, "Vt", [128, S // 128, 256], BF16)
    b_QT, b_KT, b_Vt = cbuf(c), cbuf(c), cbuf(c)
    pc = 0
    nb = min(512, L)
    for cl in range(r):
        for m0 in range(0, L, nb):
            def xcols(k, a0, n):
                v = xT[:, k, :].rearrange("p (m r) -> p m r", r=r)
                return v[:, a0:a0 + n, cl]
            for which, dst, b_dst in ((0, QT, b_QT), (1, KT, b_KT)):
                for fc in range(2):
                    pi = 6 + pc % 2
                    pc += 1
                    fw.mm([(lambda e, k=k, pi=pi, which=which, fc=fc: e.matmul(
                        c.ps[pi][:, 0:nb], lhsT=wa[:, k, which, fc * 128:(fc + 1) * 128], rhs=xcols(k, m0, nb),
                        start=(k == 0), stop=(k == 7))) for k in range(8)], reads=[b_xT, b_wa], writes=[c.b_ps[pi]])
                    eng = fw.ACT if fc == 0 else fw.DVE
                    if fc == 0:
                        fw.op(fw.ACT, lambda e, pi=pi, dst=dst, fc=fc: e.activation(out=dst[:, fc, cl * L + m0:cl * L + m0 + nb],
                                                                                    in_=c.ps[pi][:, 0:nb], func=AF.Copy),
                              reads=[c.b_ps[pi]], writes=[b_dst])
                    else:
                        fw.op(fw.DVE, lambda e, pi=pi, dst=dst, fc=fc: e.tensor_copy(out=dst[:, fc, cl * L + m0:cl * L + m0 + nb],
                                                                                     in_=c.ps[pi][:, 0:nb]),
                              reads=[c.b_ps[pi]], writes=[b_dst])
            for a0 in range(m0, m0 + nb, 128):
                pi = 6 + pc % 2
                pc += 1
                fw.mm([(lambda e, k=k, pi=pi, a0=a0: e.matmul(c.ps[pi][:, 0:256], lhsT=xcols(k, a0, 128), rhs=wa[:, k, 2, :],
                                                              start=(k == 0), stop=(k == 7))) for k in range(8)],
                      reads=[b_xT, b_wa], writes=[c.b_ps[pi]])
                fw.op(fw.ACT, lambda e, pi=pi, a0=a0: e.activation(out=Vt[:, (cl * L + a0) // 128, :], in_=c.ps[pi][:, 0:256], func=AF.Copy),
                      reads=[c.b_ps[pi]], writes=[b_Vt])
    prefetch()
    WIN = 2048
    Ust = sb(c, "Ust", [128, 2, WIN], F32)
    Dst = sb(c, "Dst", [128, 2, WIN], F32)
    b_Ust = c.dmabuf("Ust")
    b_Dst = c.dmabuf("Dst")
    NS = 4
    tmp = [sb(c, "atmp%d" % i, [128, 512], F32) for i in range(NS)]
    PT = [sb(c, "aPT%d" % i, [128, 512], BF16) for i in range(NS)]
    b_tmp = [cbuf(c) for i in range(NS)]
    b_PT = [cbuf(c) for i in range(NS)]
    row0 = g * 256
    Uv = c.U32[row0:row0 + 256, :].rearrange("(s p) t -> p s t", p=128)
    Dv = c.DEN[row0:row0 + 256, :].rearrange("(s p) t -> p s t", p=128)

    def flush(wstart, wend):
        n = wend - wstart
        tiles = list(range((tok0 + wstart) // TT, (tok0 + wend - 1) // TT + 1))
        fw.dma(fw.SP, [(Uv[:, :, tok0 + wstart:tok0 + wend], Ust[:, :, 0:n])], reads=[b_Ust], writes=[c.dr_U32[t] for t in tiles], semb=b_Ust)
        fw.dma(fw.SP, [(Dv[:, :, tok0 + wstart:tok0 + wend], Dst[:, :, 0:n])], reads=[b_Dst], writes=[c.dr_U32[t] for t in tiles], semb=b_Dst)

    units = []
    wstart = None
    for j in range(nkt + 1):
        nq = 64 if (j == 0 or j == nkt) else 128
        mstart = max(0, 128 * j - 64)
        ns, ne = mstart * r, (mstart + nq) * r
        fl = None
        if wstart is None:
            wstart, wend = ns, ns
        if ne - wstart > WIN:
            fl = (wstart, wend)
            wstart, wend = ns, ns
        soff = ns - wstart
        wend = ne
        if j == 0:
            kts = [(0, 1, slice(64, 128))]
        elif j == nkt:
            kts = [(nkt - 1, 0, slice(0, 64))]
        else:
            kts = [(j - 1, 0, slice(0, 128)), (j, 1, slice(0, 128))]
        for cl in range(r):
            units.append(dict(nq=nq, mstart=mstart, soff=soff, kts=kts, base=cl * L, cl=cl, flush_before=(fl if cl == 0 else None)))
    last_flush = (wstart, wend)
    scnt = [0]
    POS = (0, 2, 1, 3)

    def front(u, ui):
        nq = u["nq"]
        u["slots"] = []
        for (kti, ty, csl) in u["kts"]:
            idx = scnt[0]
            scnt[0] += 1
            si = idx % NS
            u["slots"].append(si)
            bA, bB = (0, 1) if idx % 2 == 0 else (6, 7)
            for par, bk in ((0, bA), (1, bB)):
                pst = c.ps[bk][:, 0:2 * nq]
                fw.mm([(lambda e, hh=hh, pst=pst, kti=kti, q=q: e.matmul(
                    pst[:, q * nq:(q + 1) * nq], lhsT=KT[par * 64:par * 64 + 64, hh // 2, u["base"] + kti * 128:u["base"] + (kti + 1) * 128],
                    rhs=QT[par * 64:par * 64 + 64, hh // 2, u["base"] + u["mstart"]:u["base"] + u["mstart"] + nq],
                    start=True, stop=True)) for q, hh in enumerate((par, par + 2))], reads=[b_KT, b_QT], writes=[c.b_ps[bk]])
                fw.op(fw.DVE, lambda e, pst=pst, si=si, ty=ty, csl=csl, par=par: e.scalar_tensor_tensor(
                    out=tmp[si][:, par * 2 * nq:(par + 1) * 2 * nq].rearrange("p (h q) -> p h q", h=2),
                    in0=pst.rearrange("p (h q) -> p h q", h=2), scalar=0.125,
                    in1=c.tabs[:, g * 8 + par * 2 + ty:g * 8 + 8:4, csl], op0=ALU.mult, op1=ALU.add),
                    reads=[c.b_ps[bk], c.b_tabs], writes=[b_tmp[si]])
            fw.op(fw.ACT, lambda e, si=si: e.activation(out=PT[si][:, 0:4 * nq], in_=tmp[si][:, 0:4 * nq], func=AF.Exp),
                  reads=[b_tmp[si]], writes=[b_PT[si]])

    def back(u, ui):
        nq = u["nq"]
        if u["flush_before"] is not None:
            flush(*u["flush_before"])
        pu = c.ps[2 + ui % 2]
        pd = c.ps[4 + ui % 2]
        b_pu, b_pd = c.b_ps[2 + ui % 2], c.b_ps[4 + ui % 2]
        nk = len(u["kts"])
        fns = []
        for hh in range(4):
            for ki, (kti, ty, csl) in enumerate(u["kts"]):
                si = u["slots"][ki]
                vtile = (u["base"] + kti * 128) // 128
                fns.append(lambda e, hh=hh, ki=ki, si=si, vtile=vtile: e.matmul(
                    pu[(hh % 2) * 64:(hh % 2) * 64 + 64, (hh // 2) * 128:(hh // 2) * 128 + nq], lhsT=Vt[:, vtile, hh * 64:(hh + 1) * 64],
                    rhs=PT[si][:, POS[hh] * nq:(POS[hh] + 1) * nq], start=(ki == 0), stop=(ki == nk - 1)))
        fw.mm(fns, reads=[b_Vt] + [b_PT[si] for si in u["slots"]], writes=[b_pu])
        fns = []
        for hh in range(4):
            for ki in range(nk):
                si = u["slots"][ki]
                fns.append(lambda e, hh=hh, ki=ki, si=si: e.matmul(
                    pd[(hh % 2) * 64:(hh % 2) * 64 + 64, (hh // 2) * 128:(hh // 2) * 128 + nq], lhsT=c.ones16[:, 0:64],
                    rhs=PT[si][:, POS[hh] * nq:(POS[hh] + 1) * nq], start=(ki == 0), stop=(ki == nk - 1)))
        fw.mm(fns, reads=[c.b_const] + [b_PT[si] for si in u["slots"]], writes=[b_pd])
        soff, cl = u["soff"], u["cl"]
        uv = Ust[:, :, soff:soff + nq * r].rearrange("p s (m r) -> p s m r", r=r)[:, :, :, cl]
        dv = Dst[:, :, soff:soff + nq * r].rearrange("p s (m r) -> p s m r", r=r)[:, :, :, cl]
        fw.op(fw.ACT, lambda e: e.activation(out=uv, in_=pu[:, 0:256].rearrange("p (s q) -> p s q", s=2)[:, :, 0:nq], func=AF.Copy),
              reads=[b_pu], writes=[b_Ust])
        fw.op(fw.DVE, lambda e: e.tensor_copy(out=dv, in_=pd[:, 0:256].rearrange("p (s q) -> p s q", s=2)[:, :, 0:nq]),
              reads=[b_pd], writes=[b_Dst])

    for ui, u in enumerate(units):
        front(u, ui)
        if ui >= 1:
            back(units[ui - 1], ui - 1)
    back(units[-1], len(units) - 1)
    flush(*last_flush)


def l0a_wload(c, kind, idx, W, b_W):
    fw = c.fw
    for k in range(8):
        st, b_st = c.wstage[k % 2], c.b_wstage[k % 2]
        if kind == "h":
            wsrc = c.din["w_in_ab"][0].rearrange("(k p) (j n) -> p k j n", p=128, n=128)
            stv = st[:, 0:640].rearrange("p (j n) -> p j n", n=128)
            fw.dma(fw.SP, [(stv[:, a, :], wsrc[:, k, 4 * a + idx, :]) for a in range(5)], writes=[b_st], semb=b_st)
            fw.op(fw.POOL, lambda e, k=k, st=st: e.tensor_copy(out=W[:, k, 0:640], in_=st[:, 0:640]), reads=[b_st], writes=[b_W])
        else:
            wsrc = c.din["w_in_ab"][0].rearrange("(k p) n -> p k n", p=128)
            stv = st[:, 0:768].rearrange("p (j n) -> p j n", n=256)
            fw.dma(fw.SP, [(stv[:, a, :], wsrc[:, k, 2560 + a * 768 + idx * 256:2560 + a * 768 + (idx + 1) * 256]) for a in range(3)],
                   writes=[b_st], semb=b_st)
            fw.op(fw.POOL, lambda e, k=k, st=st: e.tensor_copy(out=W[:, k, 0:768], in_=st[:, 0:768]), reads=[b_st], writes=[b_W])


def phase_L0A(c):
    fw, nc = c.fw, c.nc
    cache = {}

    def dmabuf(name):
        if name not in cache:
            cache[name] = fw.buf(name, dma=True)
        return cache[name]

    c.dmabuf = dmabuf
    l0a_setup(c)
    SMAX = max(c.seqs)
    xT = sb(c, "xTfull", [128, 8, SMAX], BF16)
    b_xT = fw.buf()
    W = [sb(c, "Wslot%d" % i, [128, 8, 768], BF16) for i in range(2)]
    b_W = [fw.buf() for i in range(2)]
    xrows = c.din["x"].rearrange("(n p) d -> n p d", p=128)
    units = []
    for si, S in enumerate(c.seqs):
        units += [(si, "h", i) for i in range(4) if not SKIP_HGRN] + [(si, "g", i) for i in range(3) if not SKIP_ATT]
    if units:
        l0a_wload(c, units[0][1], units[0][2], W[0], b_W[0])
    ui = 0
    tok0 = 0
    for si, S in enumerate(c.seqs):
        with ExitStackLocal(c):
            xin = [sb(c, "xin%d" % i, [128, D], F32) for i in range(4)]
            b_xin = [fw.buf() for i in range(4)]
            b_xTw = [b_xT, b_xT]
            for j in range(S // 128):
                s = j % 2
                s4 = j % 4
                fw.dma(fw.SP if s == 0 else fw.ACT, [(xin[s4], xrows[tok0 // 128 + j])], writes=[b_xin[s4]], semb=c.b_wstage[s])
                fw.op(fw.ACT, lambda e, s=s, s4=s4: e.activation(out=c.ln_xb[s], in_=xin[s4], func=AF.Copy), reads=[b_xin[s4]], writes=[c.b_lnxb[s]])
                pt = c.ps_tr2[s]
                fw.mm([(lambda e, k=k, pt=pt, s=s: e.transpose(out=pt[:, k * 128:(k + 1) * 128], in_=c.ln_xb[s][:, k * 128:(k + 1) * 128],
                                                               identity=c.ident)) for k in range(8)],
                      reads=[c.b_lnxb[s], c.b_const], writes=[c.b_ps_tr2[s]])
                fw.op(fw.DVE, lambda e, j=j, pt=pt: e.tensor_copy(out=xT[:, :, j * 128:(j + 1) * 128], in_=pt.rearrange("p (k t) -> p k t", k=8)),
                      reads=[c.b_ps_tr2[s]], writes=[b_xTw[s]])
            fw.barrier()
        for kind_ in ("h", "g"):
            us = [u for u in units if u[0] == si and u[1] == kind_]
            if not us:
                continue
            with ExitStackLocal(c):
                c.replay = {"sb": [], "buf": [], "i_sb": 0, "i_buf": 0}
                for (sj, kind, idx) in us:
                    slot = ui % 2
                    c.replay["i_sb"] = 0
                    c.replay["i_buf"] = 0

                    def prefetch(ui=ui):
                        rp, c.replay = c.replay, None
                        if ui + 1 < len(units):
                            l0a_wload(c, units[ui + 1][1], units[ui + 1][2], W[(ui + 1) % 2], b_W[(ui + 1) % 2])
                        c.replay = rp

                    if kind == "h":
                        hgrn_head(c, idx, S, tok0, xT, b_xT, W[slot][:, :, 0:640].rearrange("p k (j n) -> p k j n", n=128), b_W[slot], prefetch)
                    else:
                        attn_group(c, idx, S, tok0, xT, b_xT, W[slot][:, :, 0:768].rearrange("p k (j n) -> p k j n", n=256), b_W[slot], prefetch)
                    ui += 1
                c.replay = None
                fw.barrier()
        tok0 += S


SEQS = [2048] * 4 + [4096] * 2
_NC_CACHE = {}


def kernel(**inputs):
    n = 8
    xp = np.ascontiguousarray(np.asarray(inputs["x_prompt"], dtype=np.float32))
    xs = np.ascontiguousarray(np.asarray(inputs["x_sample"], dtype=np.float32))
    if "nc" not in _NC_CACHE:
        _NC_CACHE["nc"] = build(SEQS)
    nc = _NC_CACHE["nc"]
    oh = onehot_const()
    shared = {"oh": oh}
    for name, shp in WNAMES:
        shared[name] = np.ascontiguousarray(np.asarray(inputs[name], dtype=np.float32).reshape(shp))
    in_maps = []
    for i in range(n):
        xc = np.concatenate([xp[4 * i:4 * i + 4].reshape(-1, D), xs[2 * i:2 * i + 2].reshape(-1, D)], axis=0)
        m = dict(shared)
        m["x"] = xc
        in_maps.append(m)
    res = run_bass_kernel_spmd(nc, in_maps, core_ids=list(range(n)))
    yp = np.empty((32, 2048, D), np.float32)
    ys = np.empty((16, 4096, D), np.float32)
    for i in range(n):
        y = np.asarray(res.results[i]["y"])
        yp[4 * i:4 * i + 4] = y[:8192].reshape(4, 2048, D)
        ys[2 * i:2 * i + 2] = y[8192:].reshape(2, 4096, D)
    return (yp, ys)
```

```python
import numpy as np
import concourse.bass as bass
import concourse.mybir as mybir
from concourse.bass_utils import run_bass_kernel_spmd

F32 = mybir.dt.float32
BF16 = mybir.dt.bfloat16
AF = mybir.ActivationFunctionType
ALU = mybir.AluOpType
AX = mybir.AxisListType

D = 1024
DFF = 4096
ALPHA = 4.0 ** 0.25
LN_EPS = 1e-5
RMS_EPS = 1e-6
TT = 256
NEG = -30000.0


class Eng:
    def __init__(self, fw, name, eng, own_wait):
        self.name = name
        self.eng = eng
        self.count = 0
        self.waited = {}
        self.sem = fw.nc.alloc_semaphore(name="sem_" + name)
        self.own_wait = own_wait


class Buf:
    __slots__ = ("name", "w", "r", "sem", "cnt", "multi")

    def __init__(self, name="", sem=None, multi=False):
        self.name = name
        self.multi = multi
        self.w = {}
        self.r = {}
        self.sem = sem
        self.cnt = 0


class FW:
    def __init__(self, nc):
        self.nc = nc
        self.PE = Eng(self, "pe", nc.tensor, False)
        self.ACT = Eng(self, "act", nc.scalar, True)
        self.DVE = Eng(self, "dve", nc.vector, True)
        self.POOL = Eng(self, "pool", nc.gpsimd, True)
        self.SP = Eng(self, "sp", nc.sync, True)
        self.engs = [self.PE, self.ACT, self.DVE, self.POOL, self.SP]
        self.dbufs = []
        self.nsem = 0
        self.pool = []
        self.phase_bufs = []
        self.quiet = {}
        self.rec = None

    def buf(self, name="", dma=False, multi=False):
        if not dma:
            return Buf(name, None, multi)
        if self.pool:
            sem, cnt = self.pool.pop()
        else:
            sem = self.nc.alloc_semaphore(name="dsem_%d" % self.nsem)
            self.nsem += 1
            cnt = 0
        b = Buf(name, sem)
        b.cnt = cnt
        self.dbufs.append(b)
        self.phase_bufs.append(b)
        return b

    def end_phase(self):
        self.barrier()
        for b in self.phase_bufs:
            self.pool.append((b.sem, b.cnt))
            self.dbufs.remove(b)
        self.phase_bufs = []

    def _wait(self, E, tok):
        sem, val = tok
        if sem is E.sem and not E.own_wait:
            return
        key = id(sem)
        if E.waited.get(key, 0) >= val:
            return
        E.waited[key] = val
        E.eng.wait_ge(sem, val)

    def _deps(self, E, reads, writes):
        for b in reads:
            for t in b.w.values():
                self._wait(E, t)
        for b in writes:
            for t in b.w.values():
                self._wait(E, t)
            for t in b.r.values():
                self._wait(E, t)

    def _mark(self, tok, reads, writes):
        s, v = tok
        for b in reads:
            b.r[id(s)] = (s, v)
        for b in writes:
            if b.multi:
                b.w[id(s)] = tok
            else:
                b.w = {id(s): tok}
            b.r = {}

    def record(self, fn):
        self.rec = []
        fn()
        r, self.rec = self.rec, None
        return r

    def emit_interleaved(self, lists):
        lists = [list(l) for l in lists if l]
        pos = [0] * len(lists)
        live = True
        while live:
            live = False
            for i, l in enumerate(lists):
                if pos[i] < len(l):
                    l[pos[i]]()
                    pos[i] += 1
                    live = True

    def op(self, E, fn, reads=(), writes=()):
        if self.rec is not None:
            self.rec.append(lambda: self.op(E, fn, reads, writes))
            return
        self._deps(E, reads, writes)
        ins = fn(E.eng)
        E.count += 1
        ins.then_inc(E.sem, 1)
        self._mark((E.sem, E.count), reads, writes)

    def mm(self, fns, reads=(), writes=()):
        if self.rec is not None:
            fns = list(fns)
            self.rec.append(lambda: self.mm(fns, reads, writes))
            return
        E = self.PE
        self._deps(E, reads, writes)
        ins = None
        for fn in fns:
            ins = fn(E.eng)
        E.count += 1
        ins.then_inc(E.sem, 1)
        self._mark((E.sem, E.count), reads, writes)

    def dma(self, Q, pairs, reads=(), writes=(), semb=None, **kw):
        if self.rec is not None:
            pairs = list(pairs)
            self.rec.append(lambda: self.dma(Q, pairs, reads, writes, semb, **kw))
            return
        self._deps(Q, reads, writes)
        if semb.cnt > 0:
            self._wait(Q, (semb.sem, semb.cnt))
        for (o, i) in pairs:
            ins = Q.eng.dma_start(out=o, in_=i, **kw)
            semb.cnt += 16
            ins.then_inc(semb.sem, 16)
        self._mark((semb.sem, semb.cnt), reads, writes)

    def barrier(self):
        toks = [(E.sem, E.count) for E in self.engs if E.count > 0]
        toks += [(b.sem, b.cnt) for b in self.dbufs if b.cnt > 0]
        for E in self.engs:
            for t in toks:
                self._wait(E, t)


class Ctx:
    pass


def sb(c, name, shape, dt):
    rp = getattr(c, "replay", None)
    if rp is not None:
        i = rp["i_sb"]
        rp["i_sb"] += 1
        if i < len(rp["sb"]):
            ap, shp, d0 = rp["sb"][i]
            assert shp == list(shape) and d0 == dt, (name, shp, shape)
            return ap
        c.replay = None
        ap = sb(c, name, shape, dt)
        c.replay = rp
        rp["sb"].append((ap, list(shape), dt))
        return ap
    c.uid += 1
    h = c.es.enter_context(c.nc.sbuf_tensor("%s_%d" % (name, c.uid), list(shape), dt))
    return h.ap() if hasattr(h, "ap") else h[:]


def load_w_bf16(c, dst, src2d, semb, col_block=2048):
    fw = c.fw
    K, N = src2d.shape
    v = src2d.rearrange("(k p) n -> p k n", p=128)
    cb = min(N, 1024)
    i = 0
    for k in range(K // 128):
        for n0 in range(0, N, cb):
            st, b_st = c.wstage[i % 2], c.b_wstage[i % 2]
            i += 1
            fw.dma(fw.SP, [(st[:, 0:cb], v[:, k, n0:n0 + cb])], writes=[b_st], semb=b_st)
            fw.op(fw.POOL, lambda e, st=st, k=k, n0=n0: e.tensor_copy(out=dst[:, k, n0:n0 + cb], in_=st[:, 0:cb]),
                  reads=[b_st], writes=[semb])


def w_chunks(c, dst, src2d, b_dst):
    fw = c.fw
    K, N = src2d.shape
    v = src2d.rearrange("(k p) n -> p k n", p=128)
    out = []
    cnt = [0]
    for k in range(K // 128):
        for n0 in range(0, N, 1024):
            def thunk(k=k, n0=n0):
                st, b_st = c.wstage[cnt[0] % 2], c.b_wstage[cnt[0] % 2]
                cnt[0] += 1
                fw.dma(fw.SP, [(st[:, 0:1024], v[:, k, n0:n0 + 1024])], writes=[b_st], semb=b_st)
                fw.op(fw.POOL, lambda e: e.tensor_copy(out=dst[:, k, n0:n0 + 1024], in_=st[:, 0:1024]), reads=[b_st], writes=[b_dst])
            out.append(thunk)
    return out


def load_bc(c, dst, src1d, semb):
    fw = c.fw
    n = src1d.shape[0]
    src = bass.AP(src1d.tensor, src1d.offset, [[0, 128], [1, n]])
    fw.dma(fw.SP, [(dst, src)], writes=[semb], semb=semb)


def load_pp(c, dst, src1d, semb):
    fw = c.fw
    v = src1d.rearrange("(c p) -> p c", p=128)
    fw.dma(fw.SP, [(dst, v)], writes=[semb], semb=semb, allow_slow_non_contiguous=True)


def interleave(gens):
    gens = list(gens)
    while gens:
        for g_ in list(gens):
            try:
                next(g_)
            except StopIteration:
                gens.remove(g_)


def ln_chain(c, slot, ypre, b_ypre, gbc, bbc, b_gb, xdst_rows, xT_stage, b_xT, sub, b_dram_x, tail=True):
    fw = c.fw
    st, mv, xb, pt = c.ln_stats[slot], c.ln_mv[slot], c.ln_xb[slot], c.ps_tr2[slot]
    b_st, b_mv, b_xb, b_pt = c.b_lnst[slot], c.b_lnmv[slot], c.b_lnxb[slot], c.b_ps_tr2[slot]
    fw.op(fw.DVE, lambda e: e.bn_stats(out=st[:, 0:6], in_=ypre[:, 0:512]), reads=[b_ypre], writes=[b_st])
    yield
    fw.op(fw.DVE, lambda e: e.bn_stats(out=st[:, 6:12], in_=ypre[:, 512:1024]), reads=[b_ypre], writes=[b_st])
    yield
    fw.op(fw.DVE, lambda e: e.bn_aggr(out=mv[:, 0:2], in_=st[:, 0:12]), reads=[b_st], writes=[b_mv])
    yield
    fw.op(fw.ACT, lambda e: e.activation(out=mv[:, 3:4], in_=mv[:, 1:2], func=AF.Ln, bias=c.lneps_col[:, 0:1]),
          reads=[b_mv, c.b_const], writes=[b_mv])
    yield
    fw.op(fw.ACT, lambda e: e.activation(out=mv[:, 4:5], in_=mv[:, 3:4], func=AF.Exp, scale=-0.5), reads=[b_mv], writes=[b_mv])
    yield
    fw.op(fw.DVE, lambda e: e.tensor_scalar(out=ypre, in0=ypre, scalar1=mv[:, 0:1], scalar2=mv[:, 4:5],
                                            op0=ALU.subtract, op1=ALU.mult), reads=[b_ypre, b_mv], writes=[b_ypre])
    yield
    fw.op(fw.POOL, lambda e: e.tensor_tensor(out=ypre, in0=ypre, in1=gbc, op=ALU.mult), reads=[b_ypre, b_gb], writes=[b_ypre])
    yield
    fw.op(fw.POOL, lambda e: e.tensor_tensor(out=ypre, in0=ypre, in1=bbc, op=ALU.add), reads=[b_ypre, b_gb], writes=[b_ypre])
    yield
    fw.dma(fw.SP, [(xdst_rows, ypre)], reads=[b_ypre], writes=[b_dram_x], semb=b_ypre)
    yield
    if xT_stage is None:
        return
    fw.op(fw.ACT, lambda e: e.activation(out=xb, in_=ypre, func=AF.Copy), reads=[b_ypre], writes=[b_xb])
    yield
    if tail:
        yield from ln_tail(c, slot, xT_stage, b_xT, sub)


def ln_tail(c, slot, xT_stage, b_xT, sub):
    fw = c.fw
    xb, pt = c.ln_xb[slot], c.ps_tr2[slot]
    b_xb, b_pt = c.b_lnxb[slot], c.b_ps_tr2[slot]
    fw.mm([(lambda e, k=k: e.transpose(out=pt[:, k * 128:(k + 1) * 128], in_=xb[:, k * 128:(k + 1) * 128], identity=c.ident))
           for k in range(8)], reads=[b_xb, c.b_const], writes=[b_pt])
    yield
    fw.op(fw.DVE, lambda e: e.tensor_copy(out=xT_stage[:, :, sub * 128:(sub + 1) * 128],
                                          in_=pt.rearrange("p (k t) -> p k t", k=8)),
          reads=[b_pt], writes=[b_xT])
    yield


def setup_consts(c):
    fw, nc = c.fw, c.nc
    c.ident = sb(c, "ident", [128, 128], BF16)
    c.b_const = fw.buf("const")
    fw.op(fw.POOL, lambda e: e.memset(c.ident, 0.0), writes=[c.b_const])
    fw.op(fw.POOL, lambda e: e.affine_select(out=c.ident, in_=c.ident, pattern=[[-1, 128]], compare_op=ALU.not_equal,
                                             fill=1.0, base=0, channel_multiplier=1), reads=[c.b_const], writes=[c.b_const])
    c.ones16 = sb(c, "ones16", [128, 128], BF16)
    fw.op(fw.POOL, lambda e: e.memset(c.ones16, 1.0), writes=[c.b_const])
    c.lneps_col = sb(c, "lneps_col", [128, 1], F32)
    fw.op(fw.POOL, lambda e: e.memset(c.lneps_col, LN_EPS), writes=[c.b_const])
    c.wstage = [sb(c, "wstage%d" % i, [128, 1024], F32) for i in range(2)]
    c.b_wstage = [fw.buf("wst", dma=True) for i in range(2)]
    fw.phase_bufs = []
    c.ln_stats = [sb(c, "ln_stats%d" % i, [128, 12], F32) for i in range(2)]
    c.ln_mv = [sb(c, "ln_mv%d" % i, [128, 8], F32) for i in range(2)]
    c.ln_xb = [sb(c, "ln_xb%d" % i, [128, 1024], BF16) for i in range(2)]
    c.b_lnst = [fw.buf() for i in range(2)]
    c.b_lnmv = [fw.buf() for i in range(2)]
    c.b_lnxb = [fw.buf() for i in range(2)]
    c.ps = [nc.alloc_psum_tensor("ps%d" % i, [128, 512], F32).ap() for i in range(8)]
    c.b_ps = [fw.buf("ps%d" % i) for i in range(8)]
    c.ps_tr = c.ps[7].bitcast(BF16)
    c.b_ps_tr = c.b_ps[7]
    c.ps_tr2 = [c.ps[6].bitcast(BF16), c.ps[7].bitcast(BF16)]
    c.b_ps_tr2 = [c.b_ps[6], c.b_ps[7]]


def phase_B2(c, layer, XT_in, dr_XT_in, X_in, dr_X_in, X_out_rows, dr_X_out, XT_out, dr_XT_out, w1, b_w1, w1_todo):
    fw, nc = c.fw, c.nc
    nt = c.T // TT
    w2 = sb(c, "w2_%d" % layer, [128, 32, D], BF16)
    b_w2 = fw.buf("w2", dma=True)
    while w1_todo:
        w1_todo.pop(0)()
    load_w_bf16(c, w2, c.din["mlp_w2"][layer], b_w2, col_block=1024)
    gbc = sb(c, "b2g_%d" % layer, [128, D], F32)
    bbc = sb(c, "b2b_%d" % layer, [128, D], F32)
    b_gb = fw.buf("gb", dma=True)
    load_bc(c, gbc, c.din["ln_ffn_g"][layer], b_gb)
    load_bc(c, bbc, c.din["ln_ffn_b"][layer], b_gb)
    NB = 2
    xT = [sb(c, "b2xT%d_%d" % (i, layer), [128, 8, TT], BF16) for i in range(NB)]
    xr = [sb(c, "b2xr%d_%d" % (i, layer), [128, 2, D], F32) for i in range(NB)]
    b_xT = [fw.buf("b2xT", dma=True) for i in range(NB)]
    b_xr = [fw.buf("b2xr", dma=True) for i in range(NB)]
    h1 = sb(c, "b2h1_%d" % layer, [128, 32, TT], BF16)
    b_h1 = [fw.buf() for f in range(32)]
    rl = [sb(c, "b2rl%d_%d" % (i, layer), [128, TT], F32) for i in range(2)]
    b_rl = [fw.buf() for i in range(2)]
    yp = [sb(c, "b2yp%d_%d" % (i, layer), [128, D], F32) for i in range(2)]
    b_yp = [fw.buf("b2yp", dma=True) for i in range(2)]
    xTo = [sb(c, "b2xTo%d_%d" % (i, layer), [128, 8, TT], BF16) for i in range(1)] * 2
    b_xTo = [fw.buf("b2xTo", dma=True) for i in range(1)] * 2
    Xrows = X_in.rearrange("(n s p) d -> n p s d", s=2, p=128)

    def load(i):
        s = i % NB
        fw.dma(fw.SP, [(xT[s], XT_in[i])], reads=[dr_XT_in[i]], writes=[b_xT[s]], semb=b_xT[s])
        fw.dma(fw.SP, [(xr[s], Xrows[i])], reads=[dr_X_in[i]], writes=[b_xr[s]], semb=b_xr[s])

    def tails(i):
        interleave([ln_tail(c, 0, xTo[0], b_xTo[0], 0), ln_tail(c, 1, xTo[0], b_xTo[0], 1)])
        fw.dma(fw.SP, [(XT_out[i], xTo[0])], reads=[b_xTo[0]], writes=[dr_XT_out[i]], semb=b_xTo[0])

    load(0)
    hcnt = 0
    ycnt = 0
    for i in range(nt):
        if i + 1 < nt:
            load(i + 1)
        s = i % NB
        for f in range(32):
            pi = hcnt % 2
            hcnt += 1
            ph = c.ps[pi][:, 0:TT]
            fw.mm([(lambda e, k=k, f=f, ph=ph: e.matmul(ph, lhsT=w1[:, k, f * 128:(f + 1) * 128], rhs=xT[s][:, k, :],
                                                        start=(k == 0), stop=(k == 7))) for k in range(8)],
                  reads=[b_w1, b_xT[s]], writes=[c.b_ps[pi]])
            fw.op(fw.ACT, lambda e, ph=ph, pi=pi: e.activation(out=rl[pi], in_=ph, func=AF.Relu),
                  reads=[c.b_ps[pi]], writes=[b_rl[pi]])
            fw.op(fw.DVE, lambda e, pi=pi, f=f: e.tensor_tensor(out=h1[:, f, :], in0=rl[pi], in1=rl[pi], op=ALU.mult),
                  reads=[b_rl[pi]], writes=[b_h1[f]])
        if XT_out is not None and i > 0:
            tails(i - 1)

        def subgen(sub, i=i, s=s):
            yi = sub
            for half in range(2):
                pj = 2 + sub * 2 + half
                py = c.ps[pj]
                fw.mm([(lambda e, f=f, py=py, half=half, sub=sub: e.matmul(
                    py, lhsT=h1[:, f, sub * 128:(sub + 1) * 128], rhs=w2[:, f, half * 512:(half + 1) * 512],
                    start=(f == 0), stop=(f == 31))) for f in range(32)],
                    reads=[b_w2] + b_h1, writes=[c.b_ps[pj]])
                yield
                fw.op(fw.DVE, lambda e, py=py, half=half, sub=sub, yi=yi: e.scalar_tensor_tensor(
                    out=yp[yi][:, half * 512:(half + 1) * 512], in0=xr[s][:, sub, half * 512:(half + 1) * 512],
                    scalar=ALPHA, in1=py, op0=ALU.mult, op1=ALU.add),
                    reads=[c.b_ps[pj], b_xr[s]], writes=[b_yp[yi]])
                yield
            t0 = i * TT + sub * 128
            yield from ln_chain(c, sub, yp[yi], b_yp[yi], gbc, bbc, b_gb, X_out_rows(t0),
                                None if XT_out is None else xTo[0], b_xTo[0], sub, dr_X_out[i], tail=False)

        interleave([subgen(0), subgen(1)])
    if XT_out is not None:
        tails(nt - 1)


def phase_B1(c, layer, X_res, dr_Xres, X1, dr_X1, X1T, dr_X1T, bg=None):
    fw, nc = c.fw, c.nc
    nt = c.T // TT
    KC = 10 if layer == 0 else 8
    wname = "w_out_ab" if layer == 0 else "w_out_c"
    wo = sb(c, "wo_%d" % layer, [128, KC, D], BF16)
    b_wo = fw.buf("wo", dma=True)
    load_w_bf16(c, wo, c.din[wname][0], b_wo, col_block=1024)
    gbc = sb(c, "b1g_%d" % layer, [128, D], F32)
    bbc = sb(c, "b1b_%d" % layer, [128, D], F32)
    b_gb = fw.buf("gb1", dma=True)
    load_bc(c, gbc, c.din["ln_mix_g"][layer], b_gb)
    load_bc(c, bbc, c.din["ln_mix_b"][layer], b_gb)
    if layer == 1:
        obc = sb(c, "b1ob", [128, D], F32)
        load_bc(c, obc, c.din["b_out_c"][0], b_gb)
    NB = 2
    O16 = [sb(c, "b1O%d_%d" % (i, layer), [128, KC, TT], BF16) for i in range(NB)]
    b_O16 = [fw.buf("b1O", dma=True) for i in range(NB)]
    xr = [sb(c, "b1xr%d_%d" % (i, layer), [128, 2, D], F32) for i in range(NB)]
    b_xr = [fw.buf("b1xr", dma=True) for i in range(NB)]
    if layer == 0:
        U = [sb(c, "b1U%d" % i, [128, 6, TT], F32) for i in range(NB)]
        Dn = [sb(c, "b1D%d" % i, [128, 6, TT], F32) for i in range(NB)]
        b_U = [fw.buf("b1U", dma=True) for i in range(NB)]
        b_Dn = [fw.buf("b1Dn", dma=True) for i in range(NB)]
        dt = sb(c, "b1dt", [128, 2, TT], F32)
        b_dt = fw.buf()
    yp = [sb(c, "b1yp%d_%d" % (i, layer), [128, D], F32) for i in range(4)]
    b_yp = [fw.buf("b1yp", dma=True) for i in range(4)]
    xTo = [sb(c, "b1xTo%d_%d" % (i, layer), [128, 8, TT], BF16) for i in range(2)]
    b_xTo = [fw.buf("b1xTo", dma=True) for i in range(2)]
    Xrows = X_res.rearrange("(n s p) d -> n p s d", s=2, p=128)
    X1rows = X1.rearrange("(n s p) d -> n s p d", s=2, p=128)

    def load(i):
        s = i % NB
        if layer == 0:
            fw.dma(fw.SP, [(O16[s][:, 0:4, :], c.OTa[i])], reads=[c.dr_OTa[i]], writes=[b_O16[s]], semb=b_O16[s])
            fw.dma(fw.ACT, [(U[s], c.U32.rearrange("(cc p) t -> p cc t", p=128)[:, :, i * TT:(i + 1) * TT])], reads=[c.dr_U32[i]], writes=[b_U[s]], semb=b_U[s])
            fw.dma(fw.ACT, [(Dn[s], c.DEN.rearrange("(cc p) t -> p cc t", p=128)[:, :, i * TT:(i + 1) * TT])], reads=[c.dr_U32[i]], writes=[b_Dn[s]], semb=b_Dn[s])
        else:
            fw.dma(fw.SP, [(O16[s], c.VT[i])], reads=[c.dr_VT[i]], writes=[b_O16[s]], semb=b_O16[s])
        fw.dma(fw.SP, [(xr[s], Xrows[i])], reads=[dr_Xres[i]], writes=[b_xr[s]], semb=b_xr[s])

    def stageA(i):
        s = i % NB
        if layer == 0:
            fw.op(fw.DVE, lambda e: e.tensor_tensor(out=dt, in0=Dn[s][:, 0:2, :], in1=Dn[s][:, 2:4, :], op=ALU.add),
                  reads=[b_Dn[s]], writes=[b_dt])
            fw.op(fw.DVE, lambda e: e.tensor_tensor(out=dt, in0=dt, in1=Dn[s][:, 4:6, :], op=ALU.add),
                  reads=[b_Dn[s], b_dt], writes=[b_dt])
            fw.op(fw.ACT, lambda e: e.activation(out=dt, in_=dt, func=AF.Ln), reads=[b_dt], writes=[b_dt])
            fw.op(fw.ACT, lambda e: e.activation(out=dt, in_=dt, func=AF.Exp, scale=-1.0), reads=[b_dt], writes=[b_dt])
            for g in range(3):
                fw.op(fw.DVE, lambda e, g=g: e.tensor_tensor(out=O16[s][:, 4 + 2 * g:6 + 2 * g, :], in0=U[s][:, 2 * g:2 * g + 2, :],
                                                             in1=dt, op=ALU.mult),
                      reads=[b_U[s], b_dt], writes=[b_O16[s]])

        def subgen(sub):
            yi = (i % 2) * 2 + sub
            for half in range(2):
                pj = sub * 2 + half
                py = c.ps[pj]
                fw.mm([(lambda e, k=k, py=py, half=half, sub=sub: e.matmul(
                    py, lhsT=O16[s][:, k, sub * 128:(sub + 1) * 128], rhs=wo[:, k, half * 512:(half + 1) * 512],
                    start=(k == 0), stop=(k == KC - 1))) for k in range(KC)],
                    reads=[b_wo, b_O16[s]], writes=[c.b_ps[pj]])
                yield
                fw.op(fw.DVE, lambda e, py=py, half=half, sub=sub, yi=yi: e.scalar_tensor_tensor(
                    out=yp[yi][:, half * 512:(half + 1) * 512], in0=xr[s][:, sub, half * 512:(half + 1) * 512],
                    scalar=ALPHA, in1=py, op0=ALU.mult, op1=ALU.add),
                    reads=[c.b_ps[pj], b_xr[s]], writes=[b_yp[yi]])
                yield
            if layer == 1:
                fw.op(fw.POOL, lambda e, yi=yi: e.tensor_tensor(out=yp[yi], in0=yp[yi], in1=obc, op=ALU.add),
                      reads=[b_yp[yi], b_gb], writes=[b_yp[yi]])
                yield

        interleave([subgen(0), subgen(1)])

    def stageB(i):
        so = i % 2
        interleave([ln_chain(c, sub, yp[(i % 2) * 2 + sub], b_yp[(i % 2) * 2 + sub], gbc, bbc, b_gb, X1rows[i, sub], xTo[so], b_xTo[so],
                             sub, dr_X1[i]) for sub in range(2)])
        fw.dma(fw.SP, [(X1T[i], xTo[so])], reads=[b_xTo[so]], writes=[dr_X1T[i]], semb=b_xTo[so])

    load(0)
    if nt > 1:
        load(1)
    stageA(0)
    for i in range(nt):
        if i + 2 < nt:
            load(i + 2)
        if bg and i % 2 == 1:
            bg.pop(0)()
        ls = [fw.record(lambda: stageB(i))]
        if i + 1 < nt:
            ls.append(fw.record(lambda: stageA(i + 1)))
        fw.emit_interleaved(ls)


def phase_L1A(c, XT_in, dr_XT_in, VT, dr_VT):
    fw, nc = c.fw, c.nc
    win = sb(c, "cwin", [128, 8, 2048], BF16)
    b_win = fw.buf("cwin", dma=True)
    load_w_bf16(c, win, c.din["w_in_c"][0], b_win)
    b_small = fw.buf("csmall", dma=True)
    bin_ = sb(c, "cbin", [128, 16], F32)
    load_pp(c, bin_, c.din["b_in_c"][0], b_small)
    dwb = sb(c, "cdwb", [128, 8], F32)
    load_pp(c, dwb, c.din["dw_b_c"][0], b_small)
    cg = sb(c, "ccg", [128, 8], F32)
    load_pp(c, cg, c.din["cnorm_g"][0], b_small)
    cb = sb(c, "ccb", [128, 8], F32)
    load_pp(c, cb, c.din["cnorm_b"][0], b_small)
    dw = sb(c, "cdw", [128, 8, 31], F32)
    dwv = c.din["dw_c"][0].rearrange("j (c p) -> p c j", p=128)
    fw.dma(fw.SP, [(dw[:, k, :], dwv[:, k, :]) for k in range(8)], writes=[b_small], semb=b_small, allow_slow_non_contiguous=True)
    CB = 256
    diag = sb(c, "cdiag", [128, 8 * 31, 128], BF16)
    b_diag = fw.buf()
    for k in range(8):
        fw.op(fw.POOL if k % 2 else fw.DVE, lambda e, k=k: e.tensor_tensor(
            out=diag[:, k * 31:(k + 1) * 31, :], in0=c.ident.rearrange("p (o n) -> p o n", o=1).to_broadcast([128, 31, 128]),
            in1=dw[:, k, :].rearrange("p (j o) -> p j o", o=1).to_broadcast([128, 31, 128]), op=ALU.mult),
            reads=[b_small, c.b_const], writes=[b_diag])
    SMAX = max(c.seqs)
    uT = sb(c, "cuT", [128, 8, SMAX + 32], BF16)
    b_uT = fw.buf()
    NB = 2
    xt = [sb(c, "cxt%d" % i, [128, 8, TT], BF16) for i in range(NB)]
    b_xt = [fw.buf("cxt", dma=True) for i in range(NB)]
    sg = [sb(c, "csg%d" % i, [128, TT], F32) for i in range(2)]
    b_sg = [fw.buf() for i in range(2)]
    y32 = sb(c, "cy32", [128, 8, CB], F32)
    b_y32 = [fw.buf() for _ in range(8)]
    y16 = [sb(c, "cy16_%d" % i, [128, CB], BF16) for i in range(2)]
    b_y16 = [fw.buf() for i in range(2)]
    q16 = [sb(c, "cq16_%d" % i, [128, CB], BF16) for i in range(2)]
    b_q16 = [fw.buf() for i in range(2)]
    mean = sb(c, "cmean", [128, CB], F32)
    rstd = sb(c, "crstd", [128, CB], F32)
    b_mr = fw.buf()
    msq = sb(c, "cmsq", [128, CB], F32)
    b_msq = fw.buf()
    tall = sb(c, "ctall", [128, 8, CB], F32)
    b_tall = fw.buf()
    vst = [sb(c, "cvst%d" % i, [128, 8, CB], BF16) for i in range(1)]
    b_vst = [fw.buf("cvst", dma=True) for i in range(1)]
    fw.op(fw.POOL, lambda e: e.memset(uT, 0.0), writes=[b_uT])
    tok0 = 0
    cnt = 0
    vcnt = 0
    for S in c.seqs:
        nt = S // TT
        fw.op(fw.POOL, lambda e, S=S: e.memset(uT[:, :, 15 + S:15 + S + 16], 0.0), writes=[b_uT])
        ti0 = tok0 // TT
        fw.dma(fw.SP, [(xt[0], XT_in[ti0])], reads=[dr_XT_in[ti0]], writes=[b_xt[0]], semb=b_xt[0])
        for i in range(nt):
            s = i % NB
            if i + 1 < nt:
                s1 = (i + 1) % NB
                fw.dma(fw.SP, [(xt[s1], XT_in[ti0 + i + 1])], reads=[dr_XT_in[ti0 + i + 1]], writes=[b_xt[s1]], semb=b_xt[s1])
            for k in range(8):
                pa = cnt % 2
                pg = 2 + cnt % 2
                cnt += 1
                fw.mm([(lambda e, kk=kk, k=k, pa=pa: e.matmul(c.ps[pa][:, 0:TT], lhsT=win[:, kk, k * 128:(k + 1) * 128],
                                                              rhs=xt[s][:, kk, :], start=(kk == 0), stop=(kk == 7)))
                       for kk in range(8)], reads=[b_win, b_xt[s]], writes=[c.b_ps[pa]])
                fw.mm([(lambda e, kk=kk, k=k, pg=pg: e.matmul(c.ps[pg][:, 0:TT], lhsT=win[:, kk, 1024 + k * 128:1024 + (k + 1) * 128],
                                                              rhs=xt[s][:, kk, :], start=(kk == 0), stop=(kk == 7)))
                       for kk in range(8)], reads=[b_win, b_xt[s]], writes=[c.b_ps[pg]])
                si = cnt % 2
                fw.op(fw.ACT, lambda e, pg=pg, k=k, si=si: e.activation(out=sg[si], in_=c.ps[pg][:, 0:TT], func=AF.Sigmoid,
                                                                        bias=bin_[:, 8 + k:9 + k]),
                      reads=[c.b_ps[pg], b_small], writes=[b_sg[si]])
                fw.op(fw.DVE, lambda e, pa=pa, k=k, si=si, i=i: e.scalar_tensor_tensor(
                    out=uT[:, k, 15 + i * TT:15 + (i + 1) * TT], in0=c.ps[pa][:, 0:TT], scalar=bin_[:, k:k + 1], in1=sg[si],
                    op0=ALU.add, op1=ALU.mult), reads=[c.b_ps[pa], b_sg[si], b_small], writes=[b_uT])
        for tb in range(S // CB):
            pend = None
            for k in range(8):
                pc = 4 + cnt % 2
                ds = cnt % 2
                cnt += 1
                fw.mm([(lambda e, j=j, k=k, pc=pc, tb=tb: e.matmul(c.ps[pc][:, 0:CB], lhsT=diag[:, k * 31 + j, :],
                                                                    rhs=uT[:, k, tb * CB + j:tb * CB + j + CB],
                                                                    start=(j == 0), stop=(j == 30))) for j in range(31)],
                      reads=[b_diag, b_uT], writes=[c.b_ps[pc]])
                if pend is not None:
                    pend()
                fw.op(fw.ACT, lambda e, k=k, pc=pc: e.activation(out=y32[:, k, :], in_=c.ps[pc][:, 0:CB], func=AF.Identity,
                                                                 bias=dwb[:, k:k + 1]),
                      reads=[c.b_ps[pc], b_small], writes=[b_y32[k]])
                fw.op(fw.ACT, lambda e, k=k, ds=ds: e.activation(out=y16[ds], in_=y32[:, k, :], func=AF.Copy),
                      reads=[b_y32[k]], writes=[b_y16[ds]])
                fw.op(fw.ACT, lambda e, k=k, ds=ds: e.activation(out=q16[ds], in_=y32[:, k, :], func=AF.Square),
                      reads=[b_y32[k]], writes=[b_q16[ds]])

                def pend(k=k, ds=ds):
                    fw.mm([lambda e: e.matmul(c.ps[6][:, 0:CB], lhsT=c.ones16, rhs=y16[ds], start=(k == 0), stop=(k == 7))],
                          reads=[b_y16[ds], c.b_const], writes=[c.b_ps[6]])
                    fw.mm([lambda e: e.matmul(c.ps[7][:, 0:CB], lhsT=c.ones16, rhs=q16[ds], start=(k == 0), stop=(k == 7))],
                          reads=[b_q16[ds], c.b_const], writes=[c.b_ps[7]])
            pend()
            fw.op(fw.DVE, lambda e: e.tensor_scalar(out=mean, in0=c.ps[6][:, 0:CB], scalar1=1.0 / 1024, scalar2=None, op0=ALU.mult),
                  reads=[c.b_ps[6]], writes=[b_mr])
            fw.op(fw.DVE, lambda e: e.tensor_tensor(out=msq, in0=mean, in1=mean, op=ALU.mult), reads=[b_mr], writes=[b_msq])
            fw.op(fw.DVE, lambda e: e.scalar_tensor_tensor(out=rstd, in0=c.ps[7][:, 0:CB], scalar=1.0 / 1024, in1=msq,
                                                           op0=ALU.mult, op1=ALU.subtract),
                  reads=[c.b_ps[7], b_msq], writes=[b_mr])
            fw.op(fw.ACT, lambda e: e.activation(out=rstd, in_=rstd, func=AF.Ln, bias=c.lneps_col[:, 0:1]), reads=[b_mr, c.b_const], writes=[b_mr])
            fw.op(fw.ACT, lambda e: e.activation(out=rstd, in_=rstd, func=AF.Exp, scale=-0.5), reads=[b_mr], writes=[b_mr])
            bc8 = lambda ap: ap.rearrange("p (o t) -> p o t", o=1).to_broadcast([128, 8, CB])
            fw.op(fw.DVE, lambda e: e.tensor_tensor(out=tall, in0=y32, in1=bc8(mean), op=ALU.subtract), reads=b_y32 + [b_mr], writes=[b_tall])
            fw.op(fw.POOL, lambda e: e.tensor_tensor(out=tall, in0=tall, in1=bc8(rstd), op=ALU.mult), reads=[b_tall, b_mr], writes=[b_tall])
            for k in range(8):
                fw.op(fw.ACT, lambda e, k=k: e.activation(out=vst[0][:, k, :], in_=tall[:, k, :], func=AF.Silu,
                                                          scale=cg[:, k:k + 1], bias=cb[:, k:k + 1]),
                      reads=[b_tall, b_small], writes=[b_vst[0]])
            t_i = (tok0 + tb * CB) // TT
            fw.dma(fw.SP, [(VT[t_i], vst[0])], reads=[b_vst[0]], writes=[dr_VT[t_i]], semb=b_vst[0])
        tok0 += S


WNAMES = [("rel_bias", [32, 12]), ("hgrn_lb", [2, 3, 512]), ("w_in_ab", [1, 1024, 4864]), ("hgrn_norm", [1, 512]),
          ("w_out_ab", [1, 1280, 1024]), ("w_in_c", [1, 1024, 2048]), ("b_in_c", [1, 2048]), ("dw_c", [1, 31, 1024]),
          ("dw_b_c", [1, 1024]), ("cnorm_g", [1, 1024]), ("cnorm_b", [1, 1024]), ("w_out_c", [1, 1024, 1024]),
          ("b_out_c", [1, 1024]), ("ln_mix_g", [2, 1024]), ("ln_mix_b", [2, 1024]), ("mlp_w1", [2, 1024, 4096]),
          ("mlp_w2", [2, 4096, 1024]), ("ln_ffn_g", [2, 1024]), ("ln_ffn_b", [2, 1024])]


def build(seqs, phases=("L0A", "L0B1", "L0B2", "L1A", "L1B1", "L1B2"), ext=()):
    from contextlib import ExitStack
    nc = bass.Bass("TRN2", target_bir_lowering=False)
    c = Ctx()
    c.nc = nc
    c.seqs = list(seqs)
    c.T = T = sum(seqs)
    c.uid = 0
    nt = T // TT
    c.din = {}
    c.din["x"] = nc.dram_tensor("x", [T, D], F32, kind="ExternalInput").ap()
    for n, shp in WNAMES:
        c.din[n] = nc.dram_tensor(n, shp, F32, kind="ExternalInput").ap()
    c.din["oh"] = nc.dram_tensor("oh", [3, 33, 384], F32, kind="ExternalInput").ap()
    y = nc.dram_tensor("y", [T, D], F32, kind="ExternalOutput").ap()

    def scratch(name, shape, dt):
        kind = "Internal"
        if ("in:" + name) in ext:
            kind = "ExternalInput"
        if ("out:" + name) in ext:
            kind = "ExternalOutput"
        return nc.dram_tensor(name, shape, dt, kind=kind).ap()

    c.OTa = scratch("OTa", [nt, 128, 4, TT], BF16)
    c.U32 = scratch("U32", [768, T], F32)
    c.DEN = scratch("DEN", [768, T], F32)
    c.TV = scratch("TV", [3, 4, 384], F32)
    c.VT = scratch("VT", [nt, 128, 8, TT], BF16)
    X1 = scratch("X1", [T, D], F32)
    X1T = scratch("X1T", [nt, 128, 8, TT], BF16)
    X2 = scratch("X2", [T, D], F32)
    X2T = scratch("X2T", [nt, 128, 8, TT], BF16)
    c.fw = fw = FW(nc)
    mk = lambda: [fw.buf(multi=True) for _ in range(nt)]
    c.dr_OTa, c.dr_U32, c.dr_VT = mk(), mk(), mk()
    dr_x, dr_X1, dr_X1T, dr_X2, dr_X2T, dr_y = mk(), mk(), mk(), mk(), mk(), mk()
    with ExitStack() as es_g:
        c.es = es_g
        setup_consts(c)
        X1r = X1.rearrange("(n s p) d -> n s p d", s=2, p=128)
        w1s = {}
        for ph in phases:
            layer = 0 if ph.startswith("L0") else 1
            if ph.endswith("B1") and (ph[:2] + "B2") in phases:
                es_l = ExitStack()
                c.es = es_l
                w1 = sb(c, "w1_%d" % layer, [128, 8, DFF], BF16)
                b_w1 = fw.buf()
                w1s[layer] = (w1, b_w1, w_chunks(c, w1, c.din["mlp_w1"][layer], b_w1), es_l)
            with ExitStack() as es:
                c.es = es
                if ph.endswith("B2") and layer not in w1s:
                    w1 = sb(c, "w1_%d" % layer, [128, 8, DFF], BF16)
                    b_w1 = fw.buf()
                    w1s[layer] = (w1, b_w1, w_chunks(c, w1, c.din["mlp_w1"][layer], b_w1), None)
                if ph == "L0A":
                    phase_L0A(c)
                elif ph == "L0B1":
                    phase_B1(c, 0, c.din["x"], dr_x, X1, dr_X1, X1T, dr_X1T, bg=w1s[0][2] if 0 in w1s else None)
                elif ph == "L0B2":
                    X2r = X2.rearrange("(n p) d -> n p d", p=128)
                    phase_B2(c, 0, X1T, dr_X1T, X1, dr_X1, lambda t0: X2r[t0 // 128], dr_X2, X2T, dr_X2T, *w1s[0][:3])
                elif ph == "L1A":
                    phase_L1A(c, X2T, dr_X2T, c.VT, c.dr_VT)
                elif ph == "L1B1":
                    phase_B1(c, 1, X2, dr_X2, X1, dr_X1, X1T, dr_X1T, bg=w1s[1][2] if 1 in w1s else None)
                elif ph == "L1B2":
                    yr = y.rearrange("(n p) d -> n p d", p=128)
                    phase_B2(c, 1, X1T, dr_X1T, X1, dr_X1, lambda t0: yr[t0 // 128], dr_y, None, dr_y, *w1s[1][:3])
                fw.end_phase()
            if ph.endswith("B2") and w1s[layer][3] is not None:
                w1s[layer][3].close()
        fw.barrier()
    return nc


SEG = 512
GROUP_R = (1, 4, 16)
SKIP_HGRN = False
SKIP_ATT = False


def t5_buckets_np(rel):
    half = 16
    max_exact = 8
    n = np.abs(rel)
    large = max_exact + (np.log(np.maximum(n, 1) / max_exact) / np.log(1024 / max_exact) * (half - max_exact)).astype(np.int32)
    large = np.minimum(large, half - 1)
    return (np.where(rel > 0, half, 0) + np.where(n < max_exact, n, large)).astype(np.int32)


def onehot_const():
    oh = np.zeros((3, 33, 384), np.float32)
    for g, r in enumerate(GROUP_R):
        rel = np.arange(383) - 191
        b = t5_buckets_np(rel * r)
        b = np.where(np.abs(rel) <= 64, b, 32)
        oh[g, b, np.arange(383)] = 1.0
    return oh


def l0a_setup(c):
    fw, nc = c.fw, c.nc
    b_s = fw.buf("l0s", dma=True)
    c.b_l0s = b_s
    lbraw = sb(c, "lbraw", [128, 8, 3], F32)
    src = c.din["hgrn_lb"].rearrange("a l (h p) -> p a h l", p=128)
    fw.dma(fw.SP, [(lbraw[:, a * 4 + h, :], src[:, a, h, :]) for a in range(2) for h in range(4)], writes=[b_s], semb=b_s,
           allow_slow_non_contiguous=True)
    fw.op(fw.ACT, lambda e: e.activation(out=lbraw, in_=lbraw, func=AF.Exp), reads=[b_s], writes=[b_s])
    lsum = sb(c, "lsum", [128, 8], F32)
    fw.op(fw.DVE, lambda e: e.tensor_reduce(out=lsum, in_=lbraw, axis=AX.X, op=ALU.add), reads=[b_s], writes=[b_s])
    fw.op(fw.DVE, lambda e: e.reciprocal(out=lsum, in_=lsum), reads=[b_s], writes=[b_s])
    c.lb = sb(c, "lb", [128, 8], F32)
    c.oml = sb(c, "oml", [128, 8], F32)
    c.noml = sb(c, "noml", [128, 8], F32)
    fw.op(fw.DVE, lambda e: e.tensor_tensor(out=c.lb, in0=lbraw[:, :, 0], in1=lsum, op=ALU.mult), reads=[b_s], writes=[b_s])
    fw.op(fw.DVE, lambda e: e.tensor_scalar(out=c.oml, in0=c.lb, scalar1=-1.0, scalar2=1.0, op0=ALU.mult, op1=ALU.add),
          reads=[b_s], writes=[b_s])
    fw.op(fw.DVE, lambda e: e.tensor_scalar(out=c.noml, in0=c.lb, scalar1=-1.0, scalar2=None, op0=ALU.add), reads=[b_s], writes=[b_s])
    c.lnoml = sb(c, "lnoml", [128, 8], F32)
    fw.op(fw.ACT, lambda e: e.activation(out=c.lnoml, in_=c.oml, func=AF.Ln), reads=[b_s], writes=[b_s])
    c.one_col = sb(c, "one_col", [128, 1], F32)
    fw.op(fw.DVE, lambda e: e.memset(c.one_col, 1.0), writes=[b_s])
    c.eps_col = sb(c, "eps_col", [128, 1], F32)
    fw.op(fw.DVE, lambda e: e.memset(c.eps_col, RMS_EPS), writes=[b_s])
    c.gnorm = sb(c, "gnorm", [128, 4], F32)
    load_pp(c, c.gnorm, c.din["hgrn_norm"][0], b_s)
    c.maskF = sb(c, "maskF", [128, 64], F32)
    c.maskB = sb(c, "maskB", [128, 64], F32)
    fw.op(fw.POOL, lambda e: e.memset(c.maskF, 1.0), writes=[b_s])
    fw.op(fw.POOL, lambda e: e.memset(c.maskB, 1.0), writes=[b_s])
    for pb in (0, 64):
        fw.op(fw.POOL, lambda e, pb=pb: e.affine_select(out=c.maskF[pb:pb + 64, :], in_=c.maskF[pb:pb + 64, :], pattern=[[1, 64]],
                                                       compare_op=ALU.is_ge, fill=0.0, base=0, channel_multiplier=-1),
              reads=[b_s], writes=[b_s])
        fw.op(fw.POOL, lambda e, pb=pb: e.affine_select(out=c.maskB[pb:pb + 64, :], in_=c.maskB[pb:pb + 64, :], pattern=[[-1, 64]],
                                                       compare_op=ALU.is_ge, fill=0.0, base=0, channel_multiplier=1),
              reads=[b_s], writes=[b_s])
    c.rmask = sb(c, "rmask", [128, SEG], F32)
    fw.op(fw.POOL, lambda e: e.memset(c.rmask, 1.0), writes=[b_s])
    fw.op(fw.POOL, lambda e: e.affine_select(out=c.rmask, in_=c.rmask, pattern=[[0, SEG // 64], [1, 64]], compare_op=ALU.not_equal,
                                             fill=0.0, base=0, channel_multiplier=0), reads=[b_s], writes=[b_s])
    c.tabs = sb(c, "tabs", [128, 24, 128], F32)
    c.b_tabs = fw.buf()
    with ExitStackLocal(c) as _:
        J = sb(c, "J", [128, 128], F32)
        fw.op(fw.POOL, lambda e: e.memset(J, 0.0), writes=[b_s])
        fw.op(fw.POOL, lambda e: e.affine_select(out=J, in_=J, pattern=[[1, 128]], compare_op=ALU.not_equal, fill=1.0, base=-127,
                                                 channel_multiplier=1), reads=[b_s], writes=[b_s])
        rba = sb(c, "rba", [33, 12], F32)
        fw.dma(fw.SP, [(rba[0:32, :], c.din["rel_bias"])], writes=[b_s], semb=b_s)
        fw.op(fw.POOL, lambda e: e.memset(rba[32:33, :], NEG), writes=[b_s])
        ohs = sb(c, "ohs", [33, 3, 384], F32)
        fw.dma(fw.SP, [(ohs, c.din["oh"].rearrange("g b j -> b g j"))], writes=[b_s], semb=b_s)
        tvs = sb(c, "tvs", [4, 3, 384], F32)
        b_tv = fw.buf("tv", dma=True)
        for g in range(3):
            fw.mm([lambda e, g=g: e.matmul(c.ps[0][0:4, 0:384], lhsT=rba[:, 4 * g:4 * g + 4], rhs=ohs[:, g, :], start=True, stop=True)],
                  reads=[b_s], writes=[c.b_ps[0]])
            fw.op(fw.DVE, lambda e, g=g: e.tensor_copy(out=tvs[:, g, :], in_=c.ps[0][0:4, 0:384]), reads=[c.b_ps[0]], writes=[b_tv])
        b_tvd = fw.buf(multi=True)
        fw.dma(fw.SP, [(c.TV.rearrange("g h j -> h g j"), tvs)], reads=[b_tv], writes=[b_tvd], semb=b_tv)
        hk = [sb(c, "hk%d" % i, [128, 128], F32) for i in range(2)]
        b_hk = [fw.buf("hk", dma=True) for i in range(2)]
        n = 0
        for g in range(3):
            for hh in range(4):
                for ty in range(2):
                    s = n % 2
                    off = (g * 4 + hh) * 384 + (0 if ty == 0 else 128)
                    src = bass.AP(c.TV.tensor, c.TV.offset + off, [[1, 128], [1, 128]])
                    fw.dma(fw.SP, [(hk[s], src)], reads=[b_tvd], writes=[b_hk[s]], semb=b_hk[s])
                    pi = n % 2
                    fw.mm([lambda e, s=s, pi=pi: e.matmul(c.ps[pi][:, 0:128], lhsT=hk[s], rhs=J, start=True, stop=True)],
                          reads=[b_hk[s], b_s], writes=[c.b_ps[pi]])
                    fw.op(fw.DVE, lambda e, n=n, pi=pi: e.tensor_copy(out=c.tabs[:, n, :], in_=c.ps[pi][:, 0:128]),
                          reads=[c.b_ps[pi]], writes=[c.b_tabs])
                    n += 1
        fw.barrier()


def cbuf(c):
    rp = getattr(c, "replay", None)
    if rp is None:
        return c.fw.buf()
    i = rp["i_buf"]
    rp["i_buf"] += 1
    if i < len(rp["buf"]):
        return rp["buf"][i]
    b_ = c.fw.buf()
    rp["buf"].append(b_)
    return b_


class ExitStackLocal:
    def __init__(self, c):
        from contextlib import ExitStack
        self.c = c
        self.es = ExitStack()

    def __enter__(self):
        self.prev = self.c.es
        self.es.__enter__()
        self.c.es = self.es
        return self

    def __exit__(self, *a):
        self.c.es = self.prev
        return self.es.__exit__(*a)


def hgrn_head(c, h, S, tok0, xT, b_xT, wh, b_wh, prefetch):
    fw, nc = c.fw, c.nc
    nseg = S // SEG
    ntile = S // 128
    Vtok = sb(c, "Vtok", [128, ntile, 128], BF16)
    qs = sb(c, "qs", [128, S], BF16)
    G = sb(c, "G", [128, S], BF16)
    OF = sb(c, "OF", [128, ntile, 128], F32)
    b_V, b_qs, b_G, b_OF = cbuf(c), cbuf(c), cbuf(c), cbuf(c)
    gt = sb(c, "gt", [128, SEG], F32)
    b_gt = cbuf(c)
    pc = 0
    for j in range(ntile):
        pi = pc % 2
        pc += 1
        fw.mm([(lambda e, k=k, j=j, pi=pi: e.matmul(c.ps[pi][:, 0:128], lhsT=xT[:, k, j * 128:(j + 1) * 128], rhs=wh[:, k, 1, :],
                                                    start=(k == 0), stop=(k == 7))) for k in range(8)],
              reads=[b_xT, b_wh], writes=[c.b_ps[pi]])
        fw.op(fw.ACT, lambda e, j=j, pi=pi: e.activation(out=Vtok[:, j, :], in_=c.ps[pi][:, 0:128], func=AF.Copy),
              reads=[c.b_ps[pi]], writes=[b_V])
    for sg in range(nseg):
        cs = slice(sg * SEG, (sg + 1) * SEG)
        pi = pc % 2
        pc += 1
        fw.mm([(lambda e, k=k, pi=pi: e.matmul(c.ps[pi], lhsT=wh[:, k, 0, :], rhs=xT[:, k, cs], start=(k == 0), stop=(k == 7)))
               for k in range(8)], reads=[b_xT, b_wh], writes=[c.b_ps[pi]])
        fw.op(fw.ACT, lambda e, pi=pi: e.activation(out=qs[:, cs], in_=c.ps[pi], func=AF.Silu), reads=[c.b_ps[pi]], writes=[b_qs])
        pi = pc % 2
        pc += 1
        fw.mm([(lambda e, k=k, pi=pi: e.matmul(c.ps[pi], lhsT=wh[:, k, 4, :], rhs=xT[:, k, cs], start=(k == 0), stop=(k == 7)))
               for k in range(8)], reads=[b_xT, b_wh], writes=[c.b_ps[pi]])
        fw.op(fw.ACT, lambda e, pi=pi: e.activation(out=gt, in_=c.ps[pi], func=AF.Silu), reads=[c.b_ps[pi]], writes=[b_gt])
        fw.op(fw.DVE, lambda e: e.tensor_scalar(out=G[:, cs], in0=gt, scalar1=c.gnorm[:, h:h + 1], scalar2=None, op0=ALU.mult),
              reads=[b_gt, c.b_l0s], writes=[b_G])
    prefetch()
    SH = 256
    NJ = SH // 128
    NC = SH // 64
    nsg = S // SH
    v3 = lambda ap: ap.rearrange("p (c t) -> p c t", t=64)
    c1 = lambda ap: ap.rearrange("p (c o) -> p c o", o=1)
    bC = lambda d: 3 * d
    bA = lambda d: 3 * d + 1
    bB = lambda d: 3 * d + 2
    b_OFs = [cbuf(c) for _ in range(nsg)]
    pt = c.ps_tr

    def mkset(d):
        B = Ctx()
        for nm in ("sig", "logf", "bl", "Pb", "X", "Ep", "Em"):
            setattr(B, nm, sb(c, nm + str(d), [128, SH], F32))
            setattr(B, "b_" + nm, cbuf(c))
        B.qt = [sb(c, "qt%d_%d" % (d, i), [128, SH], BF16) for i in range(2)]
        B.kt = [sb(c, "kt%d_%d" % (d, i), [128, SH], BF16) for i in range(2)]
        B.Ktok = [sb(c, "Ktok%d_%d" % (d, i), [128, NJ, 128], BF16) for i in range(2)]
        B.A16 = [sb(c, "A16_%d_%d" % (d, i), [128, NJ, 64], BF16) for i in range(2)]
        B.Tsb = [sb(c, "Tsb%d_%d" % (d, i), [128, NC, 128], F32) for i in range(2)]
        B.sm = [sb(c, "sm%d_%d" % (d, i), [128, 40], F32) for i in range(2)]
        for nm in ("qt", "kt", "Ktok", "A16", "Tsb", "sm"):
            setattr(B, "b_" + nm, [cbuf(c), cbuf(c)])
        B.car = sb(c, "car%d" % d, [128, 2], F32)
        B.Ubuf = sb(c, "Ubuf%d" % d, [128, NC + 1, 128], F32)
        B.S16 = sb(c, "S16_%d" % d, [128, NC, 128], BF16)
        B.osum = sb(c, "osum%d" % d, [128, NJ, 128], F32)
        B.sq = sb(c, "sq%d" % d, [128, NJ, 128], F32)
        B.on16 = sb(c, "on16_%d" % d, [128, NJ, 128], BF16)
        B.ss = sb(c, "ss%d" % d, [128, 4 * NJ], F32)
        B.ost = [sb(c, "ost%d_%d" % (d, i), [128, SH], BF16) for i in range(2)]
        for nm in ("car", "U", "S16", "osum", "sq", "on", "ss"):
            setattr(B, "b_" + nm, cbuf(c))
        B.b_ost = [c.dmabuf("ost%d_%d" % (d, i)) for i in range(2)]
        B.ocnt = 0
        return B

    Bs = [mkset(0), mkset(1)]

    def prep(d, sg, pp, first):
        B = Bs[d]
        sgn = 1.0 if d == 0 else -1.0
        mask = c.maskF if d == 0 else c.maskB
        lbi = d * 4 + h
        cs = slice(sg * SH, (sg + 1) * SH)
        S_ = B.sm[pp]
        pz = c.ps[bC(d)][:, 0:SH]
        fw.mm([(lambda e, k=k: e.matmul(pz, lhsT=wh[:, k, 2 + d, :], rhs=xT[:, k, cs], start=(k == 0), stop=(k == 7)))
               for k in range(8)], reads=[b_xT, b_wh], writes=[c.b_ps[bC(d)]])
        fw.op(fw.ACT, lambda e: e.activation(out=B.sig, in_=pz, func=AF.Exp), reads=[c.b_ps[bC(d)]], writes=[B.b_sig])
        fw.op(fw.ACT, lambda e: e.activation(out=B.Ep, in_=B.sig, func=AF.Ln, bias=c.one_col[:, 0:1]), reads=[B.b_sig, c.b_l0s], writes=[B.b_Ep])
        fw.op(fw.ACT, lambda e: e.activation(out=B.sig, in_=B.Ep, func=AF.Exp, scale=-1.0), reads=[B.b_Ep], writes=[B.b_sig])
        fw.op(fw.ACT, lambda e: e.activation(out=B.logf, in_=B.sig, func=AF.Ln, scale=c.noml[:, lbi:lbi + 1], bias=c.one_col[:, 0:1]),
              reads=[B.b_sig, c.b_l0s], writes=[B.b_logf])
        fw.op(fw.DVE, lambda e: e.tensor_tensor_scan(out=B.bl, data0=c.rmask[:, 0:SH], data1=B.logf, initial=0.0, op0=ALU.mult, op1=ALU.add),
              reads=[B.b_logf, c.b_l0s], writes=[B.b_bl])
        if d == 0:
            P, b_P = B.bl, B.b_bl
        else:
            fw.op(fw.DVE, lambda e: e.tensor_tensor(out=B.Pb, in0=B.bl, in1=B.logf, op=ALU.subtract), reads=[B.b_bl, B.b_logf], writes=[B.b_Pb])
            P, b_P = B.Pb, B.b_Pb
        P3, bl3 = v3(P), v3(B.bl)
        fw.op(fw.DVE, lambda e: e.tensor_tensor(out=v3(B.X), in0=P3, in1=P3[:, :, 32:33].to_broadcast([128, NC, 64]), op=ALU.subtract),
              reads=[b_P], writes=[B.b_X])
        fw.op(fw.DVE, lambda e: e.tensor_copy(out=c1(S_[:, 0:NC]), in_=P3[:, :, 32:33]), reads=[b_P], writes=[B.b_sm[pp]])
        fw.op(fw.DVE, lambda e: e.tensor_tensor(out=c1(S_[:, 8:8 + NC]), in0=bl3[:, :, 63:64], in1=c1(S_[:, 0:NC]), op=ALU.subtract),
              reads=[B.b_bl, B.b_sm[pp]], writes=[B.b_sm[pp]])
        if first:
            fw.op(fw.DVE, lambda e: e.memset(B.car, 0.0), writes=[B.b_car])
        if d == 0:
            fw.op(fw.DVE, lambda e: e.tensor_tensor(out=S_[:, 17:16 + NC], in0=S_[:, 8:7 + NC], in1=S_[:, 1:NC], op=ALU.add),
                  reads=[B.b_sm[pp]], writes=[B.b_sm[pp]])
            fw.op(fw.DVE, lambda e: e.tensor_tensor(out=S_[:, 16:17], in0=B.car[:, 0:1], in1=S_[:, 0:1], op=ALU.add),
                  reads=[B.b_sm[pp], B.b_car], writes=[B.b_sm[pp]])
            fw.op(fw.DVE, lambda e: e.tensor_copy(out=B.car[:, 0:1], in_=S_[:, 7 + NC:8 + NC]), reads=[B.b_sm[pp]], writes=[B.b_car])
        else:
            fw.op(fw.DVE, lambda e: e.tensor_tensor(out=S_[:, 16:15 + NC], in0=S_[:, 8:7 + NC], in1=S_[:, 1:NC], op=ALU.add),
                  reads=[B.b_sm[pp]], writes=[B.b_sm[pp]])
            fw.op(fw.DVE, lambda e: e.tensor_tensor(out=S_[:, 15 + NC:16 + NC], in0=S_[:, 7 + NC:8 + NC], in1=B.car[:, 0:1], op=ALU.add),
                  reads=[B.b_sm[pp], B.b_car], writes=[B.b_sm[pp]])
            fw.op(fw.DVE, lambda e: e.tensor_copy(out=B.car[:, 0:1], in_=S_[:, 0:1]), reads=[B.b_sm[pp]], writes=[B.b_car])
        fw.op(fw.ACT, lambda e: e.activation(out=S_[:, 24:24 + NC], in_=S_[:, 16:16 + NC], func=AF.Exp), reads=[B.b_sm[pp]], writes=[B.b_sm[pp]])
        fw.op(fw.ACT, lambda e: e.activation(out=B.Ep, in_=B.X, func=AF.Exp, scale=sgn, bias=c.lnoml[:, lbi:lbi + 1]),
              reads=[B.b_X, c.b_l0s], writes=[B.b_Ep])
        fw.op(fw.ACT, lambda e: e.activation(out=B.Em, in_=B.X, func=AF.Exp, scale=-sgn), reads=[B.b_X], writes=[B.b_Em])
        fw.op(fw.DVE, lambda e: e.tensor_tensor(out=B.qt[pp], in0=qs[:, cs], in1=B.Ep, op=ALU.mult), reads=[b_qs, B.b_Ep], writes=[B.b_qt[pp]])
        fw.op(fw.POOL, lambda e: e.tensor_tensor(out=B.kt[pp], in0=B.sig, in1=B.Em, op=ALU.mult), reads=[B.b_sig, B.b_Em], writes=[B.b_kt[pp]])
        ptk = c.ps[bC(d)].bitcast(BF16)[:, 768:1024]
        fw.mm([(lambda e, j=j: e.transpose(out=ptk[:, j * 128:(j + 1) * 128], in_=B.kt[pp][:, j * 128:(j + 1) * 128], identity=c.ident))
               for j in range(NJ)], reads=[B.b_kt[pp], c.b_const], writes=[c.b_ps[bC(d)]])
        fw.op(fw.DVE, lambda e: e.tensor_copy(out=B.Ktok[pp], in_=ptk.rearrange("p (j t) -> p j t", t=128)),
              reads=[c.b_ps[bC(d)]], writes=[B.b_Ktok[pp]])
        pa = c.ps[bC(d)][:, 256:384]
        fw.mm([(lambda e, ci=ci: e.matmul(pa[(ci % 2) * 64:(ci % 2) * 64 + 64, (ci // 2) * 64:(ci // 2) * 64 + 64],
                                          lhsT=B.kt[pp][:, ci * 64:(ci + 1) * 64], rhs=B.qt[pp][:, ci * 64:(ci + 1) * 64], start=True, stop=True))
               for ci in range(NC)], reads=[B.b_kt[pp], B.b_qt[pp]], writes=[c.b_ps[bC(d)]])
        fw.op(fw.DVE, lambda e: e.tensor_tensor(out=B.A16[pp], in0=pa[:, 0:NJ * 64].rearrange("p (j t) -> p j t", t=64),
                                                in1=mask.rearrange("p (o t) -> p o t", o=1).to_broadcast([128, NJ, 64]), op=ALU.mult),
              reads=[c.b_ps[bC(d)], c.b_l0s], writes=[B.b_A16[pp]])
        fw.mm([(lambda e, ci=ci: e.matmul(c.ps[bA(d) + ci % 2][:, (ci // 2) * 128:(ci // 2 + 1) * 128],
                                          lhsT=B.Ktok[pp][(ci % 2) * 64:(ci % 2) * 64 + 64, ci // 2, :],
                                          rhs=Vtok[(ci % 2) * 64:(ci % 2) * 64 + 64, sg * NJ + ci // 2, :], start=True, stop=True))
               for ci in range(NC)], reads=[B.b_Ktok[pp], b_V], writes=[c.b_ps[bA(d)], c.b_ps[bB(d)]])
        fw.op(fw.ACT, lambda e: e.activation(out=B.Tsb[pp][:, 0:NC:2, :], in_=c.ps[bA(d)][:, 0:SH].rearrange("p (j t) -> p j t", t=128), func=AF.Copy),
              reads=[c.b_ps[bA(d)]], writes=[B.b_Tsb[pp]])
        fw.op(fw.ACT, lambda e: e.activation(out=B.Tsb[pp][:, 1:NC:2, :], in_=c.ps[bB(d)][:, 0:SH].rearrange("p (j t) -> p j t", t=128), func=AF.Copy),
              reads=[c.b_ps[bB(d)]], writes=[B.b_Tsb[pp]])

    def chain(d, sg, pp, first, combine):
        B = Bs[d]
        S_ = B.sm[pp]
        Ubuf = B.Ubuf
        if first:
            fw.op(fw.DVE, lambda e: e.memset(Ubuf[:, 0 if d == 0 else NC, :], 0.0), writes=[B.b_U])
        elif d == 0:
            fw.op(fw.DVE, lambda e: e.tensor_copy(out=Ubuf[:, 0, :], in_=Ubuf[:, NC, :]), reads=[B.b_U], writes=[B.b_U])
        else:
            fw.op(fw.DVE, lambda e: e.tensor_copy(out=Ubuf[:, NC, :], in_=Ubuf[:, 0, :]), reads=[B.b_U], writes=[B.b_U])
        order = range(NC) if d == 0 else range(NC - 1, -1, -1)
        for ci in order:
            src, dst = (ci, ci + 1) if d == 0 else (ci + 1, ci)
            fw.op(fw.DVE, lambda e, ci=ci, src=src, dst=dst: e.scalar_tensor_tensor(
                out=Ubuf[:, dst, :], in0=Ubuf[:, src, :], scalar=S_[:, 24 + ci:25 + ci], in1=B.Tsb[pp][:, ci, :], op0=ALU.mult, op1=ALU.add),
                reads=[B.b_U, B.b_sm[pp], B.b_Tsb[pp]], writes=[B.b_U])
        a0 = 0 if d == 0 else 1
        fw.op(fw.POOL, lambda e: e.tensor_tensor(out=B.S16, in0=Ubuf[:, a0:a0 + NC, :], in1=c1(S_[:, 24:24 + NC]).to_broadcast([128, NC, 128]),
                                                 op=ALU.mult), reads=[B.b_U, B.b_sm[pp]], writes=[B.b_S16])
        fns = []
        for ci in order:
            pb, j = (ci % 2) * 64, ci // 2
            po = c.ps[bA(d) + ci % 2][pb:pb + 64, 256 + j * 128:256 + (j + 1) * 128]
            fns.append(lambda e, po=po, pb=pb, j=j: e.matmul(po, lhsT=B.A16[pp][pb:pb + 64, j, :], rhs=Vtok[pb:pb + 64, sg * NJ + j, :],
                                                            start=True, stop=False))
            fns.append(lambda e, po=po, ci=ci: e.matmul(po, lhsT=B.qt[pp][:, ci * 64:(ci + 1) * 64], rhs=B.S16[:, ci, :], start=False, stop=True))
        fw.mm(fns, reads=[B.b_A16[pp], b_V, B.b_qt[pp], B.b_S16], writes=[c.b_ps[bA(d)], c.b_ps[bB(d)]])
        tl = slice(sg * NJ, (sg + 1) * NJ)
        o3 = lambda bank, lo: c.ps[bank][lo:lo + 64, 256:512].rearrange("p (j t) -> p j t", t=128)
        if not combine:
            fw.op(fw.ACT, lambda e: e.activation(out=OF[0:64, tl, :], in_=o3(bA(d), 0), func=AF.Copy), reads=[c.b_ps[bA(d)]], writes=[b_OFs[sg]])
            fw.op(fw.ACT, lambda e: e.activation(out=OF[64:128, tl, :], in_=o3(bB(d), 64), func=AF.Copy), reads=[c.b_ps[bB(d)]], writes=[b_OFs[sg]])
            return
        fw.op(fw.DVE, lambda e: e.tensor_tensor(out=B.osum[0:64], in0=OF[0:64, tl, :], in1=o3(bA(d), 0), op=ALU.add),
              reads=[c.b_ps[bA(d)], b_OFs[sg]], writes=[B.b_osum])
        fw.op(fw.DVE, lambda e: e.tensor_tensor(out=B.osum[64:128], in0=OF[64:128, tl, :], in1=o3(bB(d), 64), op=ALU.add),
              reads=[c.b_ps[bB(d)], b_OFs[sg]], writes=[B.b_osum])
        fw.op(fw.ACT, lambda e: e.activation(out=B.sq, in_=B.osum, func=AF.Square), reads=[B.b_osum], writes=[B.b_sq])
        ss = B.ss
        fw.op(fw.DVE, lambda e: e.tensor_reduce(out=ss[:, 0:NJ], in_=B.sq, axis=AX.X, op=ALU.add), reads=[B.b_sq], writes=[B.b_ss])
        fw.op(fw.ACT, lambda e: e.activation(out=ss[:, NJ:2 * NJ], in_=ss[:, 0:NJ], func=AF.Ln, scale=1.0 / 128, bias=c.eps_col[:, 0:1]),
              reads=[B.b_ss, c.b_l0s], writes=[B.b_ss])
        fw.op(fw.ACT, lambda e: e.activation(out=ss[:, 3 * NJ:4 * NJ], in_=ss[:, NJ:2 * NJ], func=AF.Exp, scale=-0.5), reads=[B.b_ss], writes=[B.b_ss])
        fw.op(fw.POOL, lambda e: e.tensor_tensor(out=B.on16, in0=B.osum, in1=c1(ss[:, 3 * NJ:4 * NJ]).to_broadcast([128, NJ, 128]), op=ALU.mult),
              reads=[B.b_osum, B.b_ss], writes=[B.b_on])
        pto = c.ps[6 + d].bitcast(BF16)[:, 0:SH]
        fw.mm([(lambda e, j=j: e.transpose(out=pto[:, j * 128:(j + 1) * 128], in_=B.on16[:, j, :], identity=c.ident))
               for j in range(NJ)], reads=[B.b_on, c.b_const], writes=[c.b_ps[6 + d]])
        os_ = B.ocnt % 2
        B.ocnt += 1
        fw.op(fw.DVE, lambda e: e.tensor_tensor(out=B.ost[os_], in0=pto, in1=G[:, sg * SH:(sg + 1) * SH], op=ALU.mult),
              reads=[c.b_ps[6 + d], b_G], writes=[B.b_ost[os_]])
        ti = (tok0 + sg * SH) // TT
        fw.dma(fw.SP, [(c.OTa[ti][:, h, :], B.ost[os_])], reads=[B.b_ost[os_]], writes=[c.dr_OTa[ti]], semb=B.b_ost[os_])

    seg_of = lambda d, i: i if d == 0 else nsg - 1 - i
    prep(0, seg_of(0, 0), 0, True)
    prep(1, seg_of(1, 0), 0, True)
    for i in range(nsg):
        ls = []
        if i + 1 < nsg:
            for d in range(2):
                ls.append(fw.record(lambda d=d: prep(d, seg_of(d, i + 1), (i + 1) % 2, False)))
        for d in range(2):
            ls.append(fw.record(lambda d=d: chain(d, seg_of(d, i), i % 2, i == 0, i >= nsg // 2)))
        fw.emit_interleaved(ls)


def attn_group(c, g, S, tok0, xT, b_xT, wa, b_wa, prefetch):
    fw, nc = c.fw, c.nc
    r = GROUP_R[g]
    L = S // r
    nkt = L // 128
    QT = sb(c, "QT", [128, 2, S], BF16)
    KT = sb(c, "KT", [128, 2, S], BF16)
    Vt = sb(c, "Vt", [128, S // 128, 256], BF16)
    b_QT, b_KT, b_Vt = cbuf(c), cbuf(c), cbuf(c)
    pc = 0
    nb = min(512, L)
    for cl in range(r):
        for m0 in range(0, L, nb):
            def xcols(k, a0, n):
                v = xT[:, k, :].rearrange("p (m r) -> p m r", r=r)
                return v[:, a0:a0 + n, cl]
            for which, dst, b_dst in ((0, QT, b_QT), (1, KT, b_KT)):
                for fc in range(2):
                    pi = 6 + pc % 2
                    pc += 1
                    fw.mm([(lambda e, k=k, pi=pi, which=which, fc=fc: e.matmul(
                        c.ps[pi][:, 0:nb], lhsT=wa[:, k, which, fc * 128:(fc + 1) * 128], rhs=xcols(k, m0, nb),
                        start=(k == 0), stop=(k == 7))) for k in range(8)], reads=[b_xT, b_wa], writes=[c.b_ps[pi]])
                    eng = fw.ACT if fc == 0 else fw.DVE
                    if fc == 0:
                        fw.op(fw.ACT, lambda e, pi=pi, dst=dst, fc=fc: e.activation(out=dst[:, fc, cl * L + m0:cl * L + m0 + nb],
                                                                                    in_=c.ps[pi][:, 0:nb], func=AF.Copy),
                              reads=[c.b_ps[pi]], writes=[b_dst])
                    else:
                        fw.op(fw.DVE, lambda e, pi=pi, dst=dst, fc=fc: e.tensor_copy(out=dst[:, fc, cl * L + m0:cl * L + m0 + nb],
                                                                                     in_=c.ps[pi][:, 0:nb]),
                              reads=[c.b_ps[pi]], writes=[b_dst])
            for a0 in range(m0, m0 + nb, 128):
                pi = 6 + pc % 2
                pc += 1
                fw.mm([(lambda e, k=k, pi=pi, a0=a0: e.matmul(c.ps[pi][:, 0:256], lhsT=xcols(k, a0, 128), rhs=wa[:, k, 2, :],
                                                              start=(k == 0), stop=(k == 7))) for k in range(8)],
                      reads=[b_xT, b_wa], writes=[c.b_ps[pi]])
                fw.op(fw.ACT, lambda e, pi=pi, a0=a0: e.activation(out=Vt[:, (cl * L + a0) // 128, :], in_=c.ps[pi][:, 0:256], func=AF.Copy),
                      reads=[c.b_ps[pi]], writes=[b_Vt])
    prefetch()
    WIN = 2048
    Ust = sb(c, "Ust", [128, 2, WIN], F32)
    Dst = sb(c, "Dst", [128, 2, WIN], F32)
    b_Ust = c.dmabuf("Ust")
    b_Dst = c.dmabuf("Dst")
    NS = 4
    tmp = [sb(c, "atmp%d" % i, [128, 512], F32) for i in range(NS)]
    PT = [sb(c, "aPT%d" % i, [128, 512], BF16) for i in range(NS)]
    b_tmp = [cbuf(c) for i in range(NS)]
    b_PT = [cbuf(c) for i in range(NS)]
    row0 = g * 256
    Uv = c.U32[row0:row0 + 256, :].rearrange("(s p) t -> p s t", p=128)
    Dv = c.DEN[row0:row0 + 256, :].rearrange("(s p) t -> p s t", p=128)

    def flush(wstart, wend):
        n = wend - wstart
        tiles = list(range((tok0 + wstart) // TT, (tok0 + wend - 1) // TT + 1))
        fw.dma(fw.SP, [(Uv[:, :, tok0 + wstart:tok0 + wend], Ust[:, :, 0:n])], reads=[b_Ust], writes=[c.dr_U32[t] for t in tiles], semb=b_Ust)
        fw.dma(fw.SP, [(Dv[:, :, tok0 + wstart:tok0 + wend], Dst[:, :, 0:n])], reads=[b_Dst], writes=[c.dr_U32[t] for t in tiles], semb=b_Dst)

    units = []
    wstart = None
    for j in range(nkt + 1):
        nq = 64 if (j == 0 or j == nkt) else 128
        mstart = max(0, 128 * j - 64)
        ns, ne = mstart * r, (mstart + nq) * r
        fl = None
        if wstart is None:
            wstart, wend = ns, ns
        if ne - wstart > WIN:
            fl = (wstart, wend)
            wstart, wend = ns, ns
        soff = ns - wstart
        wend = ne
        if j == 0:
            kts = [(0, 1, slice(64, 128))]
        elif j == nkt:
            kts = [(nkt - 1, 0, slice(0, 64))]
        else:
            kts = [(j - 1, 0, slice(0, 128)), (j, 1, slice(0, 128))]
        for cl in range(r):
            units.append(dict(nq=nq, mstart=mstart, soff=soff, kts=kts, base=cl * L, cl=cl, flush_before=(fl if cl == 0 else None)))
    last_flush = (wstart, wend)
    scnt = [0]
    POS = (0, 2, 1, 3)

    def front(u, ui):
        nq = u["nq"]
        u["slots"] = []
        for (kti, ty, csl) in u["kts"]:
            idx = scnt[0]
            scnt[0] += 1
            si = idx % NS
            u["slots"].append(si)
            bA, bB = (0, 1) if idx % 2 == 0 else (6, 7)
            for par, bk in ((0, bA), (1, bB)):
                pst = c.ps[bk][:, 0:2 * nq]
                fw.mm([(lambda e, hh=hh, pst=pst, kti=kti, q=q: e.matmul(
                    pst[:, q * nq:(q + 1) * nq], lhsT=KT[par * 64:par * 64 + 64, hh // 2, u["base"] + kti * 128:u["base"] + (kti + 1) * 128],
                    rhs=QT[par * 64:par * 64 + 64, hh // 2, u["base"] + u["mstart"]:u["base"] + u["mstart"] + nq],
                    start=True, stop=True)) for q, hh in enumerate((par, par + 2))], reads=[b_KT, b_QT], writes=[c.b_ps[bk]])
                fw.op(fw.DVE, lambda e, pst=pst, si=si, ty=ty, csl=csl, par=par: e.scalar_tensor_tensor(
                    out=tmp[si][:, par * 2 * nq:(par + 1) * 2 * nq].rearrange("p (h q) -> p h q", h=2),
                    in0=pst.rearrange("p (h q) -> p h q", h=2), scalar=0.125,
                    in1=c.tabs[:, g * 8 + par * 2 + ty:g * 8 + 8:4, csl], op0=ALU.mult, op1=ALU.add),
                    reads=[c.b_ps[bk], c.b_tabs], writes=[b_tmp[si]])
            fw.op(fw.ACT, lambda e, si=si: e.activation(out=PT[si][:, 0:4 * nq], in_=tmp[si][:, 0:4 * nq], func=AF.Exp),
                  reads=[b_tmp[si]], writes=[b_PT[si]])

    def back(u, ui):
        nq = u["nq"]
        if u["flush_before"] is not None:
            flush(*u["flush_before"])
        pu = c.ps[2 + ui % 2]
        pd = c.ps[4 + ui % 2]
        b_pu, b_pd = c.b_ps[2 + ui % 2], c.b_ps[4 + ui % 2]
        nk = len(u["kts"])
        fns = []
        for hh in range(4):
            for ki, (kti, ty, csl) in enumerate(u["kts"]):
                si = u["slots"][ki]
                vtile = (u["base"] + kti * 128) // 128
                fns.append(lambda e, hh=hh, ki=ki, si=si, vtile=vtile: e.matmul(
                    pu[(hh % 2) * 64:(hh % 2) * 64 + 64, (hh // 2) * 128:(hh // 2) * 128 + nq], lhsT=Vt[:, vtile, hh * 64:(hh + 1) * 64],
                    rhs=PT[si][:, POS[hh] * nq:(POS[hh] + 1) * nq], start=(ki == 0), stop=(ki == nk - 1)))
        fw.mm(fns, reads=[b_Vt] + [b_PT[si] for si in u["slots"]], writes=[b_pu])
        fns = []
        for hh in range(4):
            for ki in range(nk):
                si = u["slots"][ki]
                fns.append(lambda e, hh=hh, ki=ki, si=si: e.matmul(
                    pd[(hh % 2) * 64:(hh % 2) * 64 + 64, (hh // 2) * 128:(hh // 2) * 128 + nq], lhsT=c.ones16[:, 0:64],
                    rhs=PT[si][:, POS[hh] * nq:(POS[hh] + 1) * nq], start=(ki == 0), stop=(ki == nk - 1)))
        fw.mm(fns, reads=[c.b_const] + [b_PT[si] for si in u["slots"]], writes=[b_pd])
        soff, cl = u["soff"], u["cl"]
        uv = Ust[:, :, soff:soff + nq * r].rearrange("p s (m r) -> p s m r", r=r)[:, :, :, cl]
        dv = Dst[:, :, soff:soff + nq * r].rearrange("p s (m r) -> p s m r", r=r)[:, :, :, cl]
        fw.op(fw.ACT, lambda e: e.activation(out=uv, in_=pu[:, 0:256].rearrange("p (s q) -> p s q", s=2)[:, :, 0:nq], func=AF.Copy),
              reads=[b_pu], writes=[b_Ust])
        fw.op(fw.DVE, lambda e: e.tensor_copy(out=dv, in_=pd[:, 0:256].rearrange("p (s q) -> p s q", s=2)[:, :, 0:nq]),
              reads=[b_pd], writes=[b_Dst])

    for ui, u in enumerate(units):
        front(u, ui)
        if ui >= 1:
            back(units[ui - 1], ui - 1)
    back(units[-1], len(units) - 1)
    flush(*last_flush)


def l0a_wload(c, kind, idx, W, b_W):
    fw = c.fw
    for k in range(8):
        st, b_st = c.wstage[k % 2], c.b_wstage[k % 2]
        if kind == "h":
            wsrc = c.din["w_in_ab"][0].rearrange("(k p) (j n) -> p k j n", p=128, n=128)
            stv = st[:, 0:640].rearrange("p (j n) -> p j n", n=128)
            fw.dma(fw.SP, [(stv[:, a, :], wsrc[:, k, 4 * a + idx, :]) for a in range(5)], writes=[b_st], semb=b_st)
            fw.op(fw.POOL, lambda e, k=k, st=st: e.tensor_copy(out=W[:, k, 0:640], in_=st[:, 0:640]), reads=[b_st], writes=[b_W])
        else:
            wsrc = c.din["w_in_ab"][0].rearrange("(k p) n -> p k n", p=128)
            stv = st[:, 0:768].rearrange("p (j n) -> p j n", n=256)
            fw.dma(fw.SP, [(stv[:, a, :], wsrc[:, k, 2560 + a * 768 + idx * 256:2560 + a * 768 + (idx + 1) * 256]) for a in range(3)],
                   writes=[b_st], semb=b_st)
            fw.op(fw.POOL, lambda e, k=k, st=st: e.tensor_copy(out=W[:, k, 0:768], in_=st[:, 0:768]), reads=[b_st], writes=[b_W])


def phase_L0A(c):
    fw, nc = c.fw, c.nc
    cache = {}

    def dmabuf(name):
        if name not in cache:
            cache[name] = fw.buf(name, dma=True)
        return cache[name]

    c.dmabuf = dmabuf
    l0a_setup(c)
    SMAX = max(c.seqs)
    xT = sb(c, "xTfull", [128, 8, SMAX], BF16)
    b_xT = fw.buf()
    W = [sb(c, "Wslot%d" % i, [128, 8, 768], BF16) for i in range(2)]
    b_W = [fw.buf() for i in range(2)]
    xrows = c.din["x"].rearrange("(n p) d -> n p d", p=128)
    units = []
    for si, S in enumerate(c.seqs):
        units += [(si, "h", i) for i in range(4) if not SKIP_HGRN] + [(si, "g", i) for i in range(3) if not SKIP_ATT]
    if units:
        l0a_wload(c, units[0][1], units[0][2], W[0], b_W[0])
    ui = 0
    tok0 = 0
    for si, S in enumerate(c.seqs):
        with ExitStackLocal(c):
            xin = [sb(c, "xin%d" % i, [128, D], F32) for i in range(4)]
            b_xin = [fw.buf() for i in range(4)]
            b_xTw = [b_xT, b_xT]
            for j in range(S // 128):
                s = j % 2
                s4 = j % 4
                fw.dma(fw.SP if s == 0 else fw.ACT, [(xin[s4], xrows[tok0 // 128 + j])], writes=[b_xin[s4]], semb=c.b_wstage[s])
                fw.op(fw.ACT, lambda e, s=s, s4=s4: e.activation(out=c.ln_xb[s], in_=xin[s4], func=AF.Copy), reads=[b_xin[s4]], writes=[c.b_lnxb[s]])
                pt = c.ps_tr2[s]
                fw.mm([(lambda e, k=k, pt=pt, s=s: e.transpose(out=pt[:, k * 128:(k + 1) * 128], in_=c.ln_xb[s][:, k * 128:(k + 1) * 128],
                                                               identity=c.ident)) for k in range(8)],
                      reads=[c.b_lnxb[s], c.b_const], writes=[c.b_ps_tr2[s]])
                fw.op(fw.DVE, lambda e, j=j, pt=pt: e.tensor_copy(out=xT[:, :, j * 128:(j + 1) * 128], in_=pt.rearrange("p (k t) -> p k t", k=8)),
                      reads=[c.b_ps_tr2[s]], writes=[b_xTw[s]])
            fw.barrier()
        for kind_ in ("h", "g"):
            us = [u for u in units if u[0] == si and u[1] == kind_]
            if not us:
                continue
            with ExitStackLocal(c):
                c.replay = {"sb": [], "buf": [], "i_sb": 0, "i_buf": 0}
                for (sj, kind, idx) in us:
                    slot = ui % 2
                    c.replay["i_sb"] = 0
                    c.replay["i_buf"] = 0

                    def prefetch(ui=ui):
                        rp, c.replay = c.replay, None
                        if ui + 1 < len(units):
                            l0a_wload(c, units[ui + 1][1], units[ui + 1][2], W[(ui + 1) % 2], b_W[(ui + 1) % 2])
                        c.replay = rp

                    if kind == "h":
                        hgrn_head(c, idx, S, tok0, xT, b_xT, W[slot][:, :, 0:640].rearrange("p k (j n) -> p k j n", n=128), b_W[slot], prefetch)
                    else:
                        attn_group(c, idx, S, tok0, xT, b_xT, W[slot][:, :, 0:768].rearrange("p k (j n) -> p k j n", n=256), b_W[slot], prefetch)
                    ui += 1
                c.replay = None
                fw.barrier()
        tok0 += S


SEQS = [2048] * 4 + [4096] * 2
_NC_CACHE = {}


def kernel(**inputs):
    n = 8
    xp = np.ascontiguousarray(np.asarray(inputs["x_prompt"], dtype=np.float32))
    xs = np.ascontiguousarray(np.asarray(inputs["x_sample"], dtype=np.float32))
    if "nc" not in _NC_CACHE:
        _NC_CACHE["nc"] = build(SEQS)
    nc = _NC_CACHE["nc"]
    oh = onehot_const()
    shared = {"oh": oh}
    for name, shp in WNAMES:
        shared[name] = np.ascontiguousarray(np.asarray(inputs[name], dtype=np.float32).reshape(shp))
    in_maps = []
    for i in range(n):
        xc = np.concatenate([xp[4 * i:4 * i + 4].reshape(-1, D), xs[2 * i:2 * i + 2].reshape(-1, D)], axis=0)
        m = dict(shared)
        m["x"] = xc
        in_maps.append(m)
    res = run_bass_kernel_spmd(nc, in_maps, core_ids=list(range(n)))
    yp = np.empty((32, 2048, D), np.float32)
    ys = np.empty((16, 4096, D), np.float32)
    for i in range(n):
        y = np.asarray(res.results[i]["y"])
        yp[4 * i:4 * i + 4] = y[:8192].reshape(4, 2048, D)
        ys[2 * i:2 * i + 2] = y[8192:].reshape(2, 4096, D)
    return (yp, ys)
```
